# Optimizing a Trainium2 kernel written in Bass

```python
import math
import jax
import jax.numpy as jnp
from jax import lax
import numpy as np

D_MODEL = 1024
BATCH = 16
SEQ = 2048
DEPTH = 2

GRID_W = 64
CTX_LEN = 256
N_EVEN = (DEPTH + 1) // 2
N_ODD = DEPTH // 2
EPS = 1e-6

GLA_HEADS = 4
GLA_KEY_W = D_MODEL // 2
GLA_VAL_W = D_MODEL
GLA_KEY_DIM = GLA_KEY_W // GLA_HEADS
GLA_VAL_DIM = GLA_VAL_W // GLA_HEADS
GLA_GATE_RANK = 16
GLA_GATE_NORM = 16.0
HG_DIM = 128
HG_W = D_MODEL
HG_HEADS = HG_W // HG_DIM
LIN_CHUNK = 32
ATT_HEAD_DIM = 64
ATT_Q_HEADS = D_MODEL // ATT_HEAD_DIM
ATT_KV_HEADS = ATT_Q_HEADS // 4
ATT_GROUP = ATT_Q_HEADS // ATT_KV_HEADS
ATT_Q_W = ATT_Q_HEADS * ATT_HEAD_DIM
ATT_KV_W = ATT_KV_HEADS * ATT_HEAD_DIM
WINDOW = 128
ATT_BLOCK = 128
ROPE_BASE = 10000.0
NEG_INF = -1e30
S5_W = D_MODEL
S5_GROUP_CH = 16
S5_GROUPS = S5_W // S5_GROUP_CH
S5_STATE = 64

EV_SPLITS = (GLA_KEY_W, GLA_KEY_W, GLA_VAL_W, 2 * GLA_GATE_RANK, GLA_VAL_W, HG_W, 2 * HG_W, HG_W, HG_W)
EV_IN_W = sum(EV_SPLITS)
EV_OUT_W = GLA_VAL_W + HG_W
OD_SPLITS = (ATT_Q_W, ATT_KV_W, ATT_KV_W, ATT_Q_W, S5_W, S5_W)
OD_IN_W = sum(OD_SPLITS)
OD_OUT_W = ATT_Q_W + S5_W

kernel_name = 'hybrid_prefix_dit_block'


def rmsnorm(x, g):
    xf = x.astype(jnp.float32)
    return xf * lax.rsqrt(jnp.mean(xf * xf, axis=-1, keepdims=True) + EPS) * g.astype(jnp.float32)


def split_cols(p, sizes):
    return jnp.split(p, [int(i) for i in np.cumsum(sizes)[:-1]], axis=-1)


def to_heads(a, nh):
    bsz, length, width = a.shape
    return a.reshape(bsz, length, nh, width // nh).transpose(0, 2, 1, 3)


def from_heads(a):
    bsz, nh, length, d = a.shape
    return a.transpose(0, 2, 1, 3).reshape(bsz, length, nh * d)


def chunk_recurrence(q, k, v, g, s0):
    bsz, nh, length, dk = q.shape
    dv = v.shape[-1]
    n = length // LIN_CHUNK
    q, k, v, g = (a.astype(jnp.float32).reshape(bsz, nh, n, LIN_CHUNK, a.shape[-1]) for a in (q, k, v, g))
    bcum = jnp.cumsum(g, axis=3)
    blast = bcum[:, :, :, -1:, :]
    q_dec = q * jnp.exp(bcum)
    k_inv = k * jnp.exp(-bcum)
    k_tail = k * jnp.exp(blast - bcum)
    lower = jnp.tril(jnp.ones((LIN_CHUNK, LIN_CHUNK), dtype=bool))
    att = jnp.where(lower, jnp.einsum('bhncd,bhnsd->bhncs', q_dec, k_inv), 0.0)
    o_intra = jnp.einsum('bhncs,bhnsv->bhncv', att, v)

    def step(state, inp):
        q_n, k_n, v_n, dec_n = inp
        o_n = jnp.einsum('bhcd,bhdv->bhcv', q_n, state)
        state = dec_n[..., None] * state + jnp.einsum('bhcd,bhcv->bhdv', k_n, v_n)
        return state, o_n

    xs = (jnp.moveaxis(q_dec, 2, 0), jnp.moveaxis(k_tail, 2, 0), jnp.moveaxis(v, 2, 0),
          jnp.moveaxis(jnp.exp(blast[:, :, :, 0, :]), 2, 0))
    s_final, o_inter = lax.scan(step, s0.astype(jnp.float32), xs)
    o = o_intra + jnp.moveaxis(o_inter, 0, 2)
    return o.reshape(bsz, nh, length, dv), s_final


def bidirectional_scan(c_feats, l_feats, need_ctx):
    q_c, k_c, v_c, g_c = c_feats
    q_l, k_l, v_l, g_l = l_feats
    bsz, nh, _, dk = q_l.shape
    dv = v_l.shape[-1]
    zero = jnp.zeros((bsz, nh, dk, dv), jnp.float32)

    def rev(a):
        return jnp.flip(a, axis=2)

    o_cf, s_f = chunk_recurrence(q_c, k_c[0], v_c, g_c[0], zero)
    o_lf, _ = chunk_recurrence(q_l, k_l[0], v_l, g_l[0], s_f)
    o_cb, s_b = chunk_recurrence(rev(q_c), rev(k_c[1]), rev(v_c), rev(g_c[1]), zero)
    o_lb, _ = chunk_recurrence(rev(q_l), rev(k_l[1]), rev(v_l), rev(g_l[1]), s_b)
    o_c = o_cf + rev(o_cb) if need_ctx else None
    return o_c, o_lf + rev(o_lb)


def even_features(h, w_in, gk_w, gk_b, lb):
    aq, ak, av, alr, agate, bq, bf, bi, bgate = split_cols(h @ w_in, EV_SPLITS)
    lr_pair = jnp.split(alr, 2, axis=-1)
    g_gla = tuple(to_heads(jax.nn.log_sigmoid((lr_pair[d] @ gk_w[d] + gk_b[d]).astype(jnp.float32)) / GLA_GATE_NORM,
                           GLA_HEADS) for d in range(2))
    f_pair = jnp.split(bf, 2, axis=-1)
    forget = tuple(lb[d] + (1.0 - lb[d]) * jax.nn.sigmoid(f_pair[d].astype(jnp.float32)) for d in range(2))
    ak_h = to_heads(ak, GLA_HEADS)
    return {
        'gla': (to_heads(aq, GLA_HEADS) * GLA_KEY_DIM ** -0.5, (ak_h, ak_h), to_heads(av, GLA_HEADS), g_gla),
        'hgrn': (to_heads(bq, HG_HEADS), tuple(to_heads(1.0 - f, HG_HEADS) for f in forget),
                 to_heads(bi, HG_HEADS), tuple(to_heads(jnp.log(f), HG_HEADS) for f in forget)),
        'gla_gate': agate,
        'hgrn_gate': bgate,
    }


def even_out(o_gla, gate_a, o_hg, gate_b, gla_g, hg_g, w_out):
    a = from_heads(rmsnorm(o_gla, gla_g)) * jax.nn.silu(gate_a)
    b = from_heads(rmsnorm(o_hg, hg_g)) * jax.nn.silu(gate_b)
    return jnp.concatenate([a, b], axis=-1) @ w_out


def even_mixer(h_c, h_l, w_in, w_out, gk_w, gk_b, gla_g, lb, hg_g, need_ctx):
    f_c = even_features(h_c, w_in, gk_w, gk_b, lb)
    f_l = even_features(h_l, w_in, gk_w, gk_b, lb)
    gla_c, gla_l = bidirectional_scan(f_c['gla'], f_l['gla'], need_ctx)
    hg_c, hg_l = bidirectional_scan(f_c['hgrn'], f_l['hgrn'], need_ctx)
    y_l = even_out(gla_l, f_l['gla_gate'], hg_l, f_l['hgrn_gate'], gla_g, hg_g, w_out)
    y_c = even_out(gla_c, f_c['gla_gate'], hg_c, f_c['hgrn_gate'], gla_g, hg_g, w_out) if need_ctx else None
    return y_c, y_l


def axial_rope(x, row, col):
    half = ATT_HEAD_DIM // 2
    quarter = half // 2
    freqs = ROPE_BASE ** (-jnp.arange(quarter, dtype=jnp.float32) / quarter)

    def rot(xa, pos):
        ang = pos.astype(jnp.float32)[:, None] * freqs
        cos, sin = jnp.cos(ang), jnp.sin(ang)
        x1, x2 = xa[..., :quarter], xa[..., quarter:]
        return jnp.concatenate([x1 * cos - x2 * sin, x2 * cos + x1 * sin], axis=-1)

    return jnp.concatenate([rot(x[..., :half], row), rot(x[..., half:], col)], axis=-1)


def window_attention(q_c, k_c, v_c, q_l, k_l, v_l, sink, need_ctx):
    bsz, hkv, grp, length, d = q_l.shape
    lc = k_c.shape[2]
    nb = length // ATT_BLOCK
    sink = sink.astype(jnp.float32).reshape(hkv, grp)
    pad = ((0, 0), (0, 0), (ATT_BLOCK, ATT_BLOCK), (0, 0))
    kpad, vpad = jnp.pad(k_l, pad), jnp.pad(v_l, pad)

    def band(a):
        return jnp.concatenate([a[:, :, s * ATT_BLOCK: s * ATT_BLOCK + length].reshape(bsz, hkv, nb, ATT_BLOCK, d)
                                for s in range(3)], axis=3)

    kb, vb = band(kpad), band(vpad)
    qb = q_l.reshape(bsz, hkv, grp, nb, ATT_BLOCK, d)
    blk = jnp.arange(nb)[:, None, None]
    qpos = blk * ATT_BLOCK + jnp.arange(ATT_BLOCK)[None, :, None]
    kpos = (blk - 1) * ATT_BLOCK + jnp.arange(3 * ATT_BLOCK)[None, None, :]
    valid = (jnp.abs(qpos - kpos) <= WINDOW) & (kpos >= 0) & (kpos < length)
    s_band = jnp.where(valid, jnp.einsum('bhgnqd,bhnkd->bhgnqk', qb, kb).astype(jnp.float32), NEG_INF)
    s_ctx = jnp.einsum('bhgnqd,bhkd->bhgnqk', qb, k_c).astype(jnp.float32)
    s_sink = jnp.broadcast_to(sink[None, :, :, None, None, None], s_ctx.shape[:-1] + (1,))
    p = jax.nn.softmax(jnp.concatenate([s_ctx, s_band, s_sink], axis=-1), axis=-1)
    o_l = (jnp.einsum('bhgnqk,bhkd->bhgnqd', p[..., :lc], v_c)
           + jnp.einsum('bhgnqk,bhnkd->bhgnqd', p[..., lc:lc + 3 * ATT_BLOCK], vb))
    o_l = o_l.reshape(bsz, hkv, grp, length, d).transpose(0, 3, 1, 2, 4).reshape(bsz, length, ATT_Q_W)
    o_c = None
    if need_ctx:
        s_c = jnp.einsum('bhgqd,bhkd->bhgqk', q_c, k_c).astype(jnp.float32)
        sink_c = jnp.broadcast_to(sink[None, :, :, None, None], s_c.shape[:-1] + (1,))
        p_c = jax.nn.softmax(jnp.concatenate([s_c, sink_c], axis=-1), axis=-1)
        o_c = jnp.einsum('bhgqk,bhkd->bhgqd', p_c[..., :lc], v_c)
        o_c = o_c.transpose(0, 3, 1, 2, 4).reshape(bsz, lc, ATT_Q_W)
    return o_c, o_l


def _combine(e1, e2):
    a1, b1 = e1
    a2, b2 = e2
    return a1 * a2, a2 * b1 + b2


def diag_scan(abar, drive, s0):
    if s0 is not None:
        drive = drive.at[0].add(abar * s0)
    a = jnp.broadcast_to(abar, (drive.shape[0], 1) + abar.shape)
    _, states = lax.associative_scan(_combine, (a, drive), axis=0)
    return states


def s5_bidirectional(u_c, u_l, lam_re, lam_im, log_dt, b_re, b_im, c_re, c_im, d_skip, need_ctx):
    f32 = jnp.float32
    bmat = lax.complex(b_re.astype(f32), b_im.astype(f32))
    cmat = lax.complex(c_re.astype(f32), c_im.astype(f32))
    uc_cx, ul_cx = u_c.astype(jnp.complex64), u_l.astype(jnp.complex64)

    def read(states):
        return jnp.real(jnp.einsum('lbgp,ghp->blgh', states, cmat))

    y_c = d_skip.astype(f32) * u_c.astype(f32) if need_ctx else None
    y_l = d_skip.astype(f32) * u_l.astype(f32)
    for direction in range(2):
        lam = lax.complex(lam_re[direction].astype(f32), lam_im[direction].astype(f32))
        dt = jnp.exp(log_dt[direction].astype(f32))[:, None]
        abar = jnp.exp(lam * dt)
        bbar = ((abar - 1.0) / lam)[:, :, None] * bmat
        order = (lambda a: jnp.flip(a, axis=0)) if direction == 1 else (lambda a: a)
        st_c = diag_scan(abar, order(jnp.einsum('blgh,gph->lbgp', uc_cx, bbar)), None)
        st_l = order(diag_scan(abar, order(jnp.einsum('blgh,gph->lbgp', ul_cx, bbar)), st_c[-1]))
        y_l = y_l + read(st_l)
        if need_ctx:
            y_c = y_c + read(order(st_c))
    return y_c, y_l


def odd_features(h, w_in):
    q, k, v, g_att, u, g_s5 = split_cols(h @ w_in, OD_SPLITS)
    bsz, length, _ = h.shape
    q = q.reshape(bsz, length, ATT_KV_HEADS, ATT_GROUP, ATT_HEAD_DIM).transpose(0, 2, 3, 1, 4) * ATT_HEAD_DIM ** -0.5
    u = u.reshape(bsz, length, S5_GROUPS, S5_GROUP_CH)
    return q, to_heads(k, ATT_KV_HEADS), to_heads(v, ATT_KV_HEADS), g_att, u, g_s5


def odd_out(o_att, g_att, y_s5, g_s5, glu_w, w_out):
    z = jax.nn.gelu(y_s5.reshape(y_s5.shape[0], y_s5.shape[1], S5_W))
    a, b = jnp.split(z @ glu_w, 2, axis=-1)
    s5 = a * jax.nn.sigmoid(b)
    return jnp.concatenate([o_att * jax.nn.silu(g_att), s5 * jax.nn.silu(g_s5)], axis=-1) @ w_out


def odd_mixer(h_c, h_l, w_in, w_out, sink, lam_re, lam_im, log_dt, b_re, b_im, c_re, c_im, d_skip, glu_w,
              row, col, need_ctx):
    q_l, k_l, v_l, ga_l, u_l, gs_l = odd_features(h_l, w_in)
    q_c, k_c, v_c, ga_c, u_c, gs_c = odd_features(h_c, w_in)
    q_l, k_l = axial_rope(q_l, row, col), axial_rope(k_l, row, col)
    att_c, att_l = window_attention(q_c, k_c, v_c, q_l, k_l, v_l, sink, need_ctx)
    ssm_c, ssm_l = s5_bidirectional(u_c, u_l, lam_re, lam_im, log_dt, b_re, b_im, c_re, c_im, d_skip, need_ctx)
    y_l = odd_out(att_l, ga_l, ssm_l, gs_l, glu_w, w_out)
    y_c = odd_out(att_c, ga_c, ssm_c, gs_c, glu_w, w_out) if need_ctx else None
    return y_c, y_l


def setup_inputs(seed: int = 0) -> dict:
    key = jax.random.key(seed)
    ks = jax.random.split(key, 27)
    f32 = jnp.float32

    def nrm(k, shape, scale):
        return jax.random.normal(k, shape, f32) * scale

    d = D_MODEL
    g5, p5, h5 = S5_GROUPS, S5_STATE, S5_GROUP_CH
    return {
        'x': nrm(ks[0], (BATCH, SEQ, d), 1.0),
        'c': nrm(ks[1], (BATCH, d), 1.0),
        'ctx': nrm(ks[2], (BATCH, CTX_LEN, d), 1.0),
        'c_ctx': nrm(ks[3], (d,), 1.0),
        'ada_w': nrm(ks[4], (DEPTH, d, 3 * d), 0.5 * d ** -0.5),
        'ada_b': nrm(ks[5], (DEPTH, 3 * d), 0.02),
        'norm_g': 1.0 + nrm(ks[6], (DEPTH, d), 0.05),
        'final_norm_g': 1.0 + nrm(ks[7], (d,), 0.05),
        'ev_w_in': nrm(ks[8], (N_EVEN, d, EV_IN_W), d ** -0.5),
        'ev_w_out': nrm(ks[9], (N_EVEN, EV_OUT_W, d), EV_OUT_W ** -0.5),
        'gla_gk_w': nrm(ks[10], (N_EVEN, 2, GLA_GATE_RANK, GLA_KEY_W), GLA_GATE_RANK ** -0.5),
        'gla_gk_b': nrm(ks[11], (N_EVEN, 2, GLA_KEY_W), 0.1),
        'gla_norm_g': 1.0 + nrm(ks[12], (N_EVEN, GLA_VAL_DIM), 0.05),
        'hgrn_lb_raw': nrm(ks[13], (2, N_EVEN + 1, HG_W), 0.5),
        'hgrn_norm_g': 1.0 + nrm(ks[14], (N_EVEN, HG_DIM), 0.05),
        'od_w_in': nrm(ks[15], (N_ODD, d, OD_IN_W), d ** -0.5),
        'od_w_out': nrm(ks[16], (N_ODD, OD_OUT_W, d), OD_OUT_W ** -0.5),
        'attn_sink': nrm(ks[17], (N_ODD, ATT_Q_HEADS), 0.5),
        's5_lambda_re': -0.5 + nrm(ks[18], (N_ODD, 2, g5, p5), 0.01),
        's5_lambda_im': math.pi * jnp.arange(p5, dtype=f32) + nrm(ks[19], (N_ODD, 2, g5, p5), 0.01),
        's5_log_dt': jax.random.uniform(ks[20], (N_ODD, 2, g5), f32, math.log(1e-3), math.log(1e-1)),
        's5_b_re': nrm(ks[21], (N_ODD, g5, p5, h5), (2 * h5) ** -0.5),
        's5_b_im': nrm(ks[22], (N_ODD, g5, p5, h5), (2 * h5) ** -0.5),
        's5_c_re': nrm(ks[23], (N_ODD, g5, h5, p5), p5 ** -0.5),
        's5_c_im': nrm(ks[24], (N_ODD, g5, h5, p5), p5 ** -0.5),
        's5_d': nrm(ks[25], (N_ODD, g5, h5), 1.0),
        's5_glu_w': nrm(ks[26], (N_ODD, S5_W, 2 * S5_W), S5_W ** -0.5),
    }


def reference(x, c, ctx, c_ctx, ada_w, ada_b, norm_g, final_norm_g,
              ev_w_in, ev_w_out, gla_gk_w, gla_gk_b, gla_norm_g, hgrn_lb_raw, hgrn_norm_g,
              od_w_in, od_w_out, attn_sink, s5_lambda_re, s5_lambda_im, s5_log_dt,
              s5_b_re, s5_b_im, s5_c_re, s5_c_im, s5_d, s5_glu_w):
    length = x.shape[1]
    rows = length // GRID_W
    row = jnp.repeat(jnp.arange(rows), GRID_W)
    col = jnp.tile(jnp.arange(GRID_W), rows)
    lb_all = jnp.cumsum(jax.nn.softmax(hgrn_lb_raw.astype(jnp.float32), axis=1), axis=1)
    xl, xc = x, ctx
    for layer in range(DEPTH):
        need_ctx = layer < DEPTH - 1
        w_ada, b_ada = ada_w[layer], ada_b[layer]
        shift_l, scale_l, gate_l = jnp.split(jax.nn.silu(c) @ w_ada + b_ada, 3, axis=-1)
        shift_c, scale_c, gate_c = jnp.split(jax.nn.silu(c_ctx) @ w_ada + b_ada, 3, axis=-1)
        h_l = rmsnorm(xl, norm_g[layer]) * (1.0 + scale_l[:, None, :]) + shift_l[:, None, :]
        h_c = rmsnorm(xc, norm_g[layer]) * (1.0 + scale_c) + shift_c
        idx = layer // 2
        if layer % 2 == 0:
            y_c, y_l = even_mixer(h_c, h_l, ev_w_in[idx], ev_w_out[idx], gla_gk_w[idx], gla_gk_b[idx],
                                  gla_norm_g[idx], lb_all[:, idx], hgrn_norm_g[idx], need_ctx)
        else:
            y_c, y_l = odd_mixer(h_c, h_l, od_w_in[idx], od_w_out[idx], attn_sink[idx],
                                 s5_lambda_re[idx], s5_lambda_im[idx], s5_log_dt[idx],
                                 s5_b_re[idx], s5_b_im[idx], s5_c_re[idx], s5_c_im[idx], s5_d[idx],
                                 s5_glu_w[idx], row, col, need_ctx)
        xl = xl + gate_l[:, None, :] * y_l
        if need_ctx:
            xc = xc + gate_c * y_c
    return rmsnorm(xl, final_norm_g)
```

```python
import numpy as np
from contextlib import ExitStack
import concourse.bass as bass
import concourse.mybir as mybir
from concourse.bass_utils import run_bass_kernel_spmd

F32 = mybir.dt.float32
BF16 = mybir.dt.bfloat16
AF = mybir.ActivationFunctionType
ALU = mybir.AluOpType
AX = mybir.AxisListType

ENGS = ("pe", "act", "dve", "pool", "sp")
SAME_ENGINE_SYNC = True
NO_SELF_SYNC = ("pe", "act")

NT = 2304
NTILE = 18
BLOCKS = [(0, 256), (256, 768), (768, 1280), (1280, 1792), (1792, 2304)]
EPS = 1e-6


class T:
    def __init__(self, prog, name, ap_src, kind):
        self.prog = prog
        self.name = name
        self.src = ap_src
        self.kind = kind
        self.writer = None
        self.readers = []
        self.sem = None
        self.dma_count = 0
        self.root = self

    def __getitem__(self, idx):
        return self.src[idx]

    def view(self, name, idx):
        t = T(self.prog, name, self.src[idx], self.kind)
        t.root = self.root
        return t


class Prog:
    def __init__(self, nc, stack):
        self.nc = nc
        self.stack = stack
        self.q = {e: [] for e in ENGS}
        self.cnt = {e: 0 for e in ENGS}
        self.waited = {e: {} for e in ENGS}
        self.sems = {}
        self.semval = {}
        for e in ENGS:
            self.sems[e] = stack.enter_context(nc.semaphore("s_" + e))
        self.ninst = 0

    def sbuf(self, name, shape, dtype):
        t = self.stack.enter_context(self.nc.sbuf_tensor(name, list(shape), dtype))
        return T(self, name, t, "sbuf")

    def psum(self, name, shape, dtype=F32):
        t = self.stack.enter_context(self.nc.psum_tensor(name, list(shape), dtype))
        return T(self, name, t, "psum")

    def dram(self, name, shape, dtype, kind="Internal"):
        t = self.nc.dram_tensor(name, list(shape), dtype, kind=kind)
        return T(self, name, t.ap(), "dram")

    def share_sem(self, t, other):
        t.sem = self._tsem(other)
        t.sem_owner = other

    def _tsem(self, t):
        if t.sem is None:
            key = "t_" + t.name
            self.sems[key] = self.stack.enter_context(self.nc.semaphore("d_" + t.name))
            self.semval[key] = 0
            t.sem = key
        return t.sem

    def _deps(self, eng, reads, writes):
        deps = []
        for t in reads:
            if t.writer is not None:
                deps.append(t.writer)
        for t in writes:
            if t.writer is not None:
                deps.append(t.writer)
            deps.extend(t.readers)
        need = {}
        for (k, v) in deps:
            if k == eng and (not SAME_ENGINE_SYNC or eng in NO_SELF_SYNC):
                continue
            if v > need.get(k, 0):
                need[k] = v
        for k, v in need.items():
            if self.waited[eng].get(k, 0) >= v:
                continue
            self.waited[eng][k] = v
            self.q[eng].append(("wait", k, v))

    limit = None
    nlim = 0

    def op(self, eng, fn, reads=(), writes=()):
        if self.limit is not None:
            self.nlim += 1
            if self.nlim > self.limit:
                return 0
        reads = [t.root for t in reads]
        writes = [t.root for t in writes]
        writes = writes + [t for t in reads if t.kind == "psum" and t not in writes]
        self._deps(eng, reads, writes)
        self.cnt[eng] += 1
        self.ninst += 1
        idx = self.cnt[eng]
        self.q[eng].append(("op", fn, eng, 1))
        for t in writes:
            t.writer = (eng, idx)
            t.readers = []
        for t in reads:
            if t not in writes:
                t.readers.append((eng, idx))
        return idx

    def dma(self, eng, out_t, out_ap, in_t, in_ap, **kw):
        out_t = out_t.root
        in_t = in_t.root
        self._deps(eng, [in_t], [out_t])
        key = self._tsem(out_t)
        self.semval[key] += 16
        val = self.semval[key]
        self.ninst += 1
        self.q[eng].append(("dma", (lambda e: e.dma_start(out=out_ap, in_=in_ap, **kw)), key, 16))
        out_t.writer = (key, val)
        out_t.readers = []
        in_t.readers.append((key, val))

    def wait_all(self, eng, tiles):
        self._deps(eng, [t.root for t in tiles], [])

    def barrier(self):
        for e in ENGS:
            for k in list(self.sems.keys()):
                v = self.cnt[k] if k in self.cnt else self.semval[k]
                if v == 0 or (k == e and (not SAME_ENGINE_SYNC or e in NO_SELF_SYNC)):
                    continue
                if self.waited[e].get(k, 0) >= v:
                    continue
                self.waited[e][k] = v
                self.q[e].append(("wait", k, v))

    def emit(self):
        nc = self.nc
        handles = {"pe": "tensor", "act": "scalar", "dve": "vector", "pool": "gpsimd", "sp": "sync"}
        with nc.Block() as block:
            for e in ENGS:
                items = self.q[e]
                if not items:
                    continue

                def body(engh, items=items):
                    for it in items:
                        if it[0] == "wait":
                            engh.wait_ge(self.sems[it[1]], it[2])
                        else:
                            it[1](engh).then_inc(self.sems[it[2]], it[3])

                getattr(block, handles[e])(body)


class _Rec:
    def __getattr__(self, name):
        def mk(*a, **k):
            return lambda e: getattr(e, name)(*a, **k)
        return mk


C = _Rec()


def host_consts():
    s = np.arange(128)[:, None]
    c = np.arange(128)[None, :]
    same = (s // 32) == (c // 32)
    ind = ((np.arange(128)[:, None] // 32) == np.arange(4)[None, :]).astype(np.float32)
    out = {}
    for d, (le, lt) in enumerate([(lambda a, b: a <= b, lambda a, b: a > b), (lambda a, b: a >= b, lambda a, b: a < b)]):
        tri = (same & le(s, c)).astype(np.float32)
        trix = (same & lt(s, c)).astype(np.float32)
        out["tri%d" % d] = np.concatenate([tri, ind], axis=1)
        out["trix%d" % d] = trix
    out["ones"] = np.ones((128, 128), np.float32)
    out["negm"] = np.where(ind > 0, 0.0, -10000.0).astype(np.float32)
    return out


class K:
    def __init__(self, P, stop_after=None):
        self.P = P
        self.rr = 0
        self.stop_after = stop_after

    def ew(self):
        self.rr += 1
        return ("dve", "pool")[self.rr % 2]


def build_program(stop_after=None):
    nc = bass.Bass("TRN2", target_bir_lowering=False)
    st = ExitStack()
    P = Prog(nc, st)
    D = {}
    def din(name, shape, dt=F32):
        D[name] = P.dram(name, shape, dt, kind="ExternalInput")
        return D[name]
    xT = din("xT", [2, 8, 128, NT])
    cT = din("cT", [128, 8, 3])
    ada_w = din("ada_w", [2, 1024, 3072])
    ada_bT = din("ada_bT", [2, 128, 24])
    normgT = din("normgT", [2, 128, 8])
    fngT = din("fngT", [128, 8])
    ev_w_in = din("ev_w_in", [1024, 8224])
    ev_w_out = din("ev_w_out", [2048, 1024])
    gkw = din("gkw", [2, 16, 512])
    gkb = din("gkb", [2, 1, 512])
    glang = din("glang", [128, 2])
    hgng = din("hgng", [128, 1])
    lbraw = din("lbraw", [2, 2, 1024])
    lbrawT = din("lbrawT", [128, 2, 2, 8])
    consts = host_consts()
    for k, v in consts.items():
        din("c_" + k, list(v.shape))
    x1T = P.dram("x1T", [2, 8, 128, NT], F32, kind="ExternalOutput" if stop_after == 0 else "Internal")
    od_w_in = din("od_w_in", [1024, 4608])
    od_w_sw = din("od_w_sw", [1024, 1280])
    od_w_out = din("od_w_out", [2048, 1024])
    glu_w = din("glu_w", [1024, 2048])
    ropeC = din("ropeC", [64, 2048])
    ropeS = din("ropeS", [64, 2048])
    sinkT = din("sinkT", [128, 8])
    maskP_d = din("maskP", [128, 128])
    maskN_d = din("maskN", [128, 128])
    lamTre = din("lamTre", [128, 64])
    lamTim = din("lamTim", [128, 64])
    ldtT = din("ldtT", [128, 64])
    BTre = din("BTre", [128, 32, 16])
    BTim = din("BTim", [128, 32, 16])
    CTre = din("CTre", [128, 32, 16])
    CTim = din("CTim", [128, 32, 16])
    dskT = din("dskT", [128, 8])
    iota_d = din("iota", [128, 1152])
    outT = P.dram("outT", [2, 8, 128, 2048], F32, kind="ExternalOutput")
    mats_d = P.dram("mats_d", [64, 128, 512], BF16)
    zT_d = P.dram("zT_d", [8, 128, NT], BF16)
    aT_d = P.dram("aT_d", [16, 128, NT], BF16)

    ones = P.sbuf("ones", [128, 128], F32)
    tri = [P.sbuf("tri%d" % d, [128, 132], F32) for d in range(2)]
    trix = [P.sbuf("trix%d" % d, [128, 128], F32) for d in range(2)]
    for t, n in [(ones, "c_ones"), (tri[0], "c_tri0"), (tri[1], "c_tri1"), (trix[0], "c_trix0"), (trix[1], "c_trix1")]:
        P.dma("sp", t, t[:], D[n], D[n][:])
    hT = P.sbuf("hT", [128, 8, NT], BF16)
    ws = P.sbuf("ws", [128, 4608], F32)
    xblk = ws.view("xblk", (slice(None), slice(0, 2048)))
    sqb = ws.view("sqb", (slice(None), slice(2048, 4096)))
    rstd = P.sbuf("rstd", [128, 512], F32)
    tmpf = P.sbuf("tmpf", [128, 512], F32)
    tmpf2 = P.sbuf("tmpf2", [128, 512], F32)
    modL = [P.sbuf("mod%d" % i, [128, 24, 3], F32) for i in range(2)]
    mscL = [P.sbuf("msc%d" % i, [128, 8, 3], F32) for i in range(2)]
    mod, msc = modL[0], mscL[0]
    cTs = P.sbuf("cTs", [128, 8, 3], F32)
    silc = P.sbuf("silc", [128, 8, 3], F32)
    abT = P.sbuf("abT", [128, 24], F32)
    ngT = P.sbuf("ngT", [128, 8], F32)
    wada = [ws.view("wada0", (slice(None), slice(0, 3072)))] * 2
    qT = P.sbuf("qT", [128, NT], BF16)
    kT = [P.sbuf("kT%d" % d, [128, NT], BF16) for d in range(2)]
    ktok = [P.sbuf("ktok%d" % d, [128, NTILE, 128], BF16) for d in range(2)]
    vtok = P.sbuf("vtok", [128, NTILE, 256], BF16)
    gtok = [P.sbuf("gtok%d" % d, [128, NTILE, 128], F32) for d in range(2)]
    wA = P.sbuf("wA", [128, 8, 128], BF16)
    wB = P.sbuf("wB", [128, 8, 128], BF16)
    wC = P.sbuf("wC", [128, 8, 256], BF16)
    wD = P.sbuf("wD", [128, 8, 256], BF16)
    wE = P.sbuf("wE", [128, 8, 128], BF16)
    wout = T(P, "wout", hT[:].rearrange("p k n -> p (k n)")[:, 0:16384].rearrange("p (k n) -> p k n", k=16), "sbuf")
    wout.root = hT
    ablk = vtok.view("ablk", (slice(None), slice(0, 16), slice(None)))
    astage = P.sbuf("astage", [128, 512], BF16)
    lrT_all = P.sbuf("lrT", [48, NT], BF16)
    lrT = [lrT_all.view("lrT%d" % d, (slice(32 * d, 32 * d + 16), slice(None))) for d in range(2)]
    gkw_all = P.sbuf("gkw_s", [48, 512], BF16)
    gkw_s = [gkw_all.view("gkw_s%d" % d, (slice(32 * d, 32 * d + 16), slice(None))) for d in range(2)]
    gkb_s = [P.sbuf("gkb_s%d" % d, [1, 512], F32) for d in range(2)]
    lb_bc = P.sbuf("lb_bc", [128, 2, 1024], F32)
    oml_bc = P.sbuf("oml_bc", [128, 2, 1024], F32)
    lbtmp = T(P, "lbtmp", ws[:, 0:2048].rearrange("p (a b) -> p a b", a=2), "sbuf")
    lbT = P.sbuf("lbT", [128, 2, 2, 8], F32)
    omlT = P.sbuf("omlT", [128, 2, 8], F32)
    lbTt = P.sbuf("lbTt", [128, 2, 8], F32)
    glang_s = P.sbuf("glang_s", [128, 2], F32)
    hgng_s = P.sbuf("hgng_s", [128, 1], F32)
    EB = [P.sbuf("EB%d" % d, [128, 128], F32) for d in range(2)]
    EBn = [P.sbuf("EBn%d" % d, [128, 128], F32) for d in range(2)]
    decj2 = [[P.sbuf("decj%d_%d" % (d, i), [128, 4], F32) for i in range(2)] for d in range(2)]
    qd2 = [[P.sbuf("qd%d_%d" % (d, i), [128, 128], BF16) for i in range(2)] for d in range(2)]
    cur_par = [0]

    class _Par:
        def __init__(self, bufs):
            self.bufs = bufs

        def __getitem__(self, d):
            return self.bufs[d][cur_par[0]]
    decj = _Par(decj2)
    qd = _Par(qd2)
    ki = [P.sbuf("ki%d" % d, [128, 128], BF16) for d in range(2)]
    ktz = [P.sbuf("ktz%d" % d, [128, 4, 128], BF16) for d in range(2)]
    am = [P.sbuf("am%d" % d, [128, 128], BF16) for d in range(2)]
    S = [P.sbuf("S%d" % d, [128, 4, 256], F32) for d in range(2)]
    Sbf = [P.sbuf("Sbf%d" % d, [128, 4, 256], BF16) for d in range(2)]
    sgate = P.sbuf("sgate", [128, 512], F32)
    oi_sb = [P.sbuf("oi_sb%d" % d, [128, 256], F32) for d in range(2)]
    xres = P.sbuf("xres", [128, 512], F32)
    xo = P.sbuf("xo", [128, 512], F32)

    banks = [P.psum("bank%d" % i, [128, 512], F32) for i in range(8)]
    def pv(b, name, lo, hi):
        return banks[b].view(name, (slice(None), slice(lo, hi)))
    ps_b = [pv(d, "ps_b%d" % d, 0, 132) for d in range(2)]
    ps_d = [pv(d, "ps_d%d" % d, 256, 384) for d in range(2)]
    ps_a = [pv(2 + d, "ps_a%d" % d, 0, 128) for d in range(2)]
    ps_oi = [pv(4 + d, "ps_oi%d" % d, 0, 256) for d in range(2)]
    ps_oe = [pv(4 + d, "ps_oe%d" % d, 256, 512) for d in range(2)]
    ps_kv = [pv(6 + d, "ps_kv%d" % d, 0, 512) for d in range(2)]
    ps_p = [banks[6], banks[7], banks[0], banks[1], banks[2], banks[3], banks[4], banks[5]]
    pp = [0]

    pp_list = [ps_p]

    def next_pp():
        pp[0] = (pp[0] + 1) % len(pp_list[0])
        return pp_list[0][pp[0]]

    rr = [0]

    def ew():
        return "dve"

    def load_w(wt, src_t, col0, ncols, row0=0, nk=8, dcol=0):
        src = src_t[row0:row0 + nk * 128, col0:col0 + ncols].rearrange("(kt p) j -> p kt j", p=128)
        P.dma("pool", wt, wt[:, 0:nk, dcol:dcol + ncols], src_t, src)

    def proj_feat(wt, wcol0, M, blk, out_ps, nk=8, rhs_t=None, prow=0):
        c0, c1 = blk
        src = hT if rhs_t is None else rhs_t
        for kt in range(nk):
            P.op("pe", C.matmul(out_ps[prow:prow + M, 0:c1 - c0], lhsT=wt[:, kt, wcol0:wcol0 + M],
                                                 rhs=src[:, kt, c0:c1], start=(kt == 0), stop=(kt == nk - 1)),
                 [wt, src], [out_ps])

    def proj_tok(wt, wcol0, ncols, tile_i, out_ps):
        for kt in range(8):
            P.op("pe", C.matmul(out_ps[:, 0:ncols], lhsT=hT[:, kt, tile_i * 128:(tile_i + 1) * 128],
                                                 rhs=wt[:, kt, wcol0:wcol0 + ncols], start=(kt == 0), stop=(kt == 7)),
                 [wt, hT], [out_ps])

    def rstd_from_ps(ps, N, dim):
        P.op("act", C.activation(out=tmpf2[:, 0:N], in_=ps[:, 0:N], func=AF.Ln, scale=1.0 / dim, bias=epsb[:, 0:1]),
             [ps, epsb], [tmpf2])
        P.op("act", C.activation(out=rstd[:, 0:N], in_=tmpf2[:, 0:N], func=AF.Exp, scale=-0.5), [tmpf2], [rstd])

    epsb = P.sbuf("epsb", [128, 1], F32)
    P.op("pool", C.memset(epsb[:], EPS), [], [epsb])
    oneb = P.sbuf("oneb", [128, 1], F32)
    P.op("pool", C.memset(oneb[:], 1.0), [], [oneb])

    wada3 = []
    wi = [0]

    def adaln(L):
        P.dma("sp", cTs, cTs[:], cT, cT[:])
        P.dma("sp", abT, abT[:], ada_bT, ada_bT[L])
        P.dma("sp", ngT, ngT[:], normgT, normgT[L])
        P.op("act", C.activation(out=silc[:], in_=cTs[:], func=AF.Silu), [cTs], [silc])
        if not wada3:
            for i_ in range(3):
                wada3.append(T(P, "wada3_%d" % i_, ws[:, 1536 * i_:1536 * (i_ + 1)], "sbuf"))
        for kt in range(8):
            ps = next_pp()
            for half in range(2):
                wb = wada3[wi[0] % 3]; wi[0] += 1
                P.dma("sp", wb, wb[:], ada_w, ada_w[L, kt * 128:(kt + 1) * 128, half * 1536:(half + 1) * 1536])
                for j_ in range(12):
                    jt = half * 12 + j_
                    P.op("pe", C.matmul(ps[:, jt * 3:jt * 3 + 3], lhsT=wb[:, j_ * 128:(j_ + 1) * 128],
                                        rhs=silc[:, kt, :], start=True, stop=True), [wb, silc], [ps])
            if kt == 0:
                P.op("dve", C.tensor_tensor(out=mod[:], in0=ps[:, 0:72].rearrange("p (a b) -> p a b", b=3),
                                            in1=abT[:].unsqueeze(2).to_broadcast([128, 24, 3]), op=ALU.add),
                     [ps, abT], [mod])
            else:
                P.op("dve", C.tensor_tensor(out=mod[:], in0=ps[:, 0:72].rearrange("p (a b) -> p a b", b=3),
                                            in1=mod[:], op=ALU.add), [ps, mod], [mod])
        P.op("dve", C.scalar_tensor_tensor(out=msc[:], in0=mod[:, 8:16, :], scalar=1.0,
                                                     in1=ngT[:].unsqueeze(2).to_broadcast([128, 8, 3]),
                                                     op0=ALU.add, op1=ALU.mult), [mod, ngT], [msc])

    nm_t = []

    def norm_mod(xsrc, b):
        if not nm_t:
            nm_t.append(T(P, "n_x", ws[:, 0:4096].rearrange("p (k n) -> p k n", k=8), "sbuf"))
            nm_t.append(T(P, "n_sq0", gtok[0][:].rearrange("p a b -> p (a b)")[:, 0:2048].rearrange("p (k n) -> p k n", k=4), "sbuf"))
            nm_t.append(T(P, "n_sq1", gtok[1][:].rearrange("p a b -> p (a b)")[:, 0:2048].rearrange("p (k n) -> p k n", k=4), "sbuf"))
        xb_, sq0, sq1 = nm_t
        for (c0, c1) in BLOCKS:
            N = c1 - c0
            col = 2 if c0 == 0 else b
            P.dma("sp", xb_, xb_[:, :, 0:N], xsrc, xsrc[b, :, :, c0:c1].rearrange("k p n -> p k n"))
            P.op("act", C.activation(out=sq0[:, :, 0:N], in_=xb_[:, 0:4, 0:N], func=AF.Square), [xb_], [sq0])
            P.op("act", C.activation(out=sq1[:, :, 0:N], in_=xb_[:, 4:8, 0:N], func=AF.Square), [xb_], [sq1])
            ps = next_pp()
            for kt in range(8):
                sq = sq0 if kt < 4 else sq1
                P.op("pe", C.matmul(ps[:, 0:N], lhsT=ones[:], rhs=sq[:, kt % 4, 0:N], start=(kt == 0), stop=(kt == 7)), [ones, sq], [ps])
            rstd_from_ps(ps, N, 1024.0)
            for kt in range(8):
                P.op("dve", C.scalar_tensor_tensor(out=tmpf[:, 0:N], in0=xb_[:, kt, 0:N], scalar=msc[:, kt, col:col + 1],
                                                   in1=rstd[:, 0:N], op0=ALU.mult, op1=ALU.mult), [xb_, msc, rstd], [tmpf])
                P.op("act", C.activation(out=hT[:, kt, c0:c1], in_=tmpf[:, 0:N], func=AF.Identity,
                                         bias=mod[:, kt, col:col + 1], scale=1.0), [tmpf, mod], [hT])

    def scan_head(dv, kT_d, ktok_d, gtok_d):
        nm = dv // 128
        oacc = ws
        order = [list(range(NTILE)), [1, 0] + list(range(17, 1, -1))]
        P.op("pool", C.memset(oacc[:, 0:nm * NT], 0.0), [], [oacc])
        for d in range(2):
            P.op("pool", C.memset(S[d][:], 0.0), [], [S[d]])
            P.op("pool", C.memset(Sbf[d][:], 0.0), [], [Sbf[d]])
        nslot = 512 // dv

        def kv_mm(d, i, p):
            j = (p if d == 0 else 3 - p)
            sl = (p % nslot) * dv
            P.op("pe", C.matmul(ps_kv[d][:, sl:sl + dv], lhsT=ktz[d][:, j, :], rhs=vtok[:, i, 0:dv], start=True, stop=True),
                 [ktz[d], vtok], [ps_kv[d]])

        def s_upd(d, p):
            j = (p if d == 0 else 3 - p)
            sl = (p % nslot) * dv
            P.op("dve", C.scalar_tensor_tensor(out=S[d][:, (p + 1) % 4, 0:dv], in0=S[d][:, p, 0:dv],
                                               scalar=decj[d][:, j:j + 1], in1=ps_kv[d][:, sl:sl + dv],
                                               op0=ALU.mult, op1=ALU.add),
                 [S[d], decj[d], ps_kv[d]], [S[d]])

        def inter_mm(d, p):
            j = (p if d == 0 else 3 - p)
            for m in range(nm):
                P.op("pe", C.matmul(ps_oe[d][:, m * 128 + 32 * j:m * 128 + 32 * j + 32],
                                    lhsT=Sbf[d][:, p, m * 128:(m + 1) * 128],
                                    rhs=qd[d][:, 32 * j:32 * j + 32], start=True, stop=True),
                     [Sbf[d], qd[d]], [ps_oe[d]])

        def st1(d, i):
            g_ap = gtok_d[d][:, i, :]
            P.op("pe", C.matmul(ps_b[d][:], lhsT=g_ap, rhs=tri[d][:], start=True, stop=True), [gtok_d[d], tri[d]], [ps_b[d]])
            P.op("pe", C.matmul(ps_d[d][:], lhsT=trix[d][:], rhs=g_ap, start=True, stop=True), [gtok_d[d], trix[d]], [ps_d[d]])

        def st2(d, i):
            P.op("act", C.activation(out=EB[d][:], in_=ps_b[d][:, 0:128], func=AF.Exp), [ps_b[d]], [EB[d]])
            P.op("act", C.activation(out=EBn[d][:], in_=ps_b[d][:, 0:128], func=AF.Exp, scale=-1.0), [ps_b[d]], [EBn[d]])
            P.op("act", C.activation(out=decj[d][:], in_=ps_b[d][:, 128:132], func=AF.Exp), [ps_b[d]], [decj[d]])
            for j in range(4):
                P.op("act", C.activation(out=EDz[d][:, j, :], in_=ps_d[d][:], func=AF.Exp, bias=negm[:, j:j + 1], scale=1.0), [ps_d[d], negm], [EDz[d]])

        def st3(d, i):
            tsl = slice(i * 128, (i + 1) * 128)
            P.op("dve", C.tensor_tensor(out=qd[d][:], in0=qT[:, tsl], in1=EB[d][:], op=ALU.mult), [qT, EB[d]], [qd[d]])
            P.op("dve", C.tensor_tensor(out=ki[d][:], in0=kT_d[d][:, tsl], in1=EBn[d][:], op=ALU.mult), [kT_d[d], EBn[d]], [ki[d]])
            P.op("dve", C.tensor_tensor(out=ktz[d][:], in0=ktok_d[d][:, i, :].unsqueeze(1).to_broadcast([128, 4, 128]), in1=EDz[d][:], op=ALU.mult),
                 [ktok_d[d], EDz[d]], [ktz[d]])

        def st4(d, i):
            for p in range(nslot):
                kv_mm(d, i, p)
            P.op("pe", C.matmul(ps_a[d][:], lhsT=ki[d][:], rhs=qd[d][:], start=True, stop=True), [ki[d], qd[d]], [ps_a[d]])
            inter_mm(d, 0)

        def st5(d, i):
            for p in range(nslot):
                s_upd(d, p)
            P.op("dve", C.tensor_tensor(out=am[d][:], in0=ps_a[d][:], in1=tri[d][:, 0:128], op=ALU.mult), [ps_a[d], tri[d]], [am[d]])

        def st5b(d, i):
            if nslot < 4:
                for p in range(nslot, 4):
                    kv_mm(d, i, p)

        def st5c(d, i):
            if nslot < 4:
                for p in range(nslot, 4):
                    s_upd(d, p)

        def st6(d, i):
            P.op("act", C.copy(out=Sbf[d][:, :, 0:dv], in_=S[d][:, :, 0:dv]), [S[d]], [Sbf[d]])
            for m in range(nm):
                P.op("pe", C.matmul(ps_oi[d][:, m * 128:(m + 1) * 128], lhsT=vtok[:, i, m * 128:(m + 1) * 128],
                                    rhs=am[d][:], start=True, stop=True), [vtok, am[d]], [ps_oi[d]])

        def st7(d, i):
            P.op("act", C.copy(out=oi_sb[d][:, 0:dv], in_=ps_oi[d][:, 0:dv]), [ps_oi[d]], [oi_sb[d]])
            for p in range(1, 4):
                inter_mm(d, p)
            for m in range(nm):
                P.op("pool", C.tensor_tensor(out=oi_sb[d][:, m * 128:(m + 1) * 128], in0=oi_sb[d][:, m * 128:(m + 1) * 128],
                                             in1=oacc[:, m * NT + i * 128:m * NT + (i + 1) * 128], op=ALU.add),
                     [oi_sb[d], oacc], [oi_sb[d]])

        def st8(d, i):
            for m in range(nm):
                o_ap = oacc[:, m * NT + i * 128:m * NT + (i + 1) * 128]
                P.op("dve", C.tensor_tensor(out=o_ap, in0=ps_oe[d][:, m * 128:(m + 1) * 128],
                                            in1=oi_sb[d][:, m * 128:(m + 1) * 128], op=ALU.add),
                     [ps_oe[d], oi_sb[d]], [oacc])

        def run(stages, step):
            cur_par[0] = step % 2
            for stg in stages:
                for d in range(2):
                    stg(d, order[d][step])
        run((st1, st2, st3), 0)
        for step in range(NTILE):
            run((st4, st5, st5b, st5c), step)
            if step + 1 < NTILE:
                run((st1, st2, st3), step + 1)
            run((st6, st7, st8), step)

    def finalize_head(dv, wg, ng_t, kt_out0):
        nm = dv // 128
        oacc = ws
        for (c0, c1) in BLOCKS:
            N = c1 - c0
            for m in range(nm):
                P.op("dve", C.tensor_tensor(out=tmpf[:, 0:N] if m == 0 else tmpf2[:, 0:N], in0=oacc[:, m * NT + c0:m * NT + c1],
                                            in1=oacc[:, m * NT + c0:m * NT + c1], op=ALU.mult), [oacc], [tmpf if m == 0 else tmpf2])
            ps = next_pp()
            for m in range(nm):
                P.op("pe", C.matmul(ps[:, 0:N], lhsT=ones[:], rhs=(tmpf if m == 0 else tmpf2)[:, 0:N], start=(m == 0), stop=(m == nm - 1)),
                     [ones, tmpf if m == 0 else tmpf2], [ps])
            rstd_from_ps(ps, N, float(dv))
            for m in range(nm):
                P.op("dve", C.tensor_tensor(out=oacc[:, m * NT + c0:m * NT + c1], in0=oacc[:, m * NT + c0:m * NT + c1], in1=rstd[:, 0:N], op=ALU.mult),
                     [oacc, rstd], [oacc])
        for (c0, c1) in BLOCKS:
            N = c1 - c0
            for m in range(nm):
                psg = next_pp()
                proj_feat(wg, m * 128, 128, (c0, c1), psg)
                P.op("act", C.activation(out=sgate[:, 0:N], in_=psg[:, 0:N], func=AF.Silu), [psg], [sgate])
                P.op("dve", C.scalar_tensor_tensor(out=astage[:, 0:N], in0=oacc[:, m * NT + c0:m * NT + c1], scalar=ng_t[:, m:m + 1], in1=sgate[:, 0:N],
                                                   op0=ALU.mult, op1=ALU.mult), [oacc, ng_t, sgate], [astage])
                P.dma("sp", aT_d, aT_d[kt_out0 + m, :, c0:c1], astage, astage[:, 0:N])

    akt = []

    def out_proj_residual(wout_src, xsrc, xdst, b, nk):
        if not akt:
            srcs = [qT[:], kT[0][:], kT[1][:], ktok[0][:].rearrange("p a b -> p (a b)")]
            for si, ap_ in enumerate(srcs):
                for q4 in range(4):
                    akt.append(T(P, "akt%d" % (si * 4 + q4), ap_[:, 512 * q4:512 * (q4 + 1)], "sbuf"))
        load_w(wout, wout_src, 0, 1024, 0, nk)
        for (c0, c1) in BLOCKS:
            N = c1 - c0
            col = 2 if c0 == 0 else b
            for kt in range(nk):
                P.dma("sp", akt[kt], akt[kt][:, 0:N], aT_d, aT_d[kt, :, c0:c1])
            for jt in range(8):
                ps = next_pp()
                for kt in range(nk):
                    P.op("pe", C.matmul(ps[:, 0:N], lhsT=wout[:, kt, jt * 128:(jt + 1) * 128], rhs=akt[kt][:, 0:N],
                                        start=(kt == 0), stop=(kt == nk - 1)), [wout, akt[kt]], [ps])
                P.dma("sp", xres, xres[:, 0:N], xsrc, xsrc[b, jt, :, c0:c1])
                P.op("dve", C.scalar_tensor_tensor(out=xo[:, 0:N], in0=ps[:, 0:N], scalar=mod[:, 16 + jt, col:col + 1],
                                                   in1=xres[:, 0:N], op0=ALU.mult, op1=ALU.add), [ps, mod, xres], [xo])
                P.dma("sp", xdst, xdst[b, jt, :, c0:c1], xo, xo[:, 0:N])

    def layer0(b):
        norm_mod(xT, b)
        P.barrier()
        load_w(wA, ev_w_in, 2048, 32)
        for d in range(2):
            for blk in BLOCKS:
                ps = next_pp()
                proj_feat(wA, 16 * d, 16, blk, ps, prow=32 * d)
                P.op("act", C.copy(out=lrT[d][:, blk[0]:blk[1]], in_=ps[32 * d:32 * d + 16, 0:blk[1] - blk[0]]), [ps], [lrT[d]])
        hw = []
        for hh_ in range(4):
            hw.append(([(wA, hh_ * 128, 128), (wB, 512 + hh_ * 128, 128), (wC, 1024 + hh_ * 256, 256)], [(wD, 2080 + hh_ * 256, 256)]))
        for hh_ in range(8):
            hw.append(([(wA, 3104 + hh_ * 128, 128), (wC, 4128 + hh_ * 128, 128), (wC, 5152 + hh_ * 128, 128, 128), (wB, 6176 + hh_ * 128, 128)],
                       [(wD, 7200 + hh_ * 128, 128)]))

        def wload(idx, late):
            if idx < len(hw):
                for ent in hw[idx][1 if late else 0]:
                    load_w(ent[0], ev_w_in, ent[1], ent[2], dcol=(ent[3] if len(ent) > 3 else 0))
        wload(0, False)
        wload(0, True)
        for hh in range(4):
            for blk in BLOCKS:
                N = blk[1] - blk[0]
                ps = next_pp()
                proj_feat(wA, 0, 128, blk, ps)
                P.op("act", C.activation(out=qT[:, blk[0]:blk[1]], in_=ps[:, 0:N], func=AF.Copy, scale=128.0 ** -0.5), [ps], [qT])
                ps = next_pp()
                proj_feat(wB, 0, 128, blk, ps)
                P.op("dve", C.tensor_copy(out=kT[0][:, blk[0]:blk[1]], in_=ps[:, 0:N]), [ps], [kT[0]])
            for i in range(NTILE):
                ps = next_pp()
                proj_tok(wB, 0, 128, i, ps)
                P.op("act", C.copy(out=ktok[0][:, i, :], in_=ps[:, 0:128]), [ps], [ktok[0]])
                ps = next_pp()
                proj_tok(wC, 0, 256, i, ps)
                P.op("dve", C.tensor_copy(out=vtok[:, i, :], in_=ps[:, 0:256]), [ps], [vtok])
                for d in range(2):
                    ps = next_pp()
                    P.op("pe", C.matmul(ps[:, 0:128], lhsT=lrT[d][:, i * 128:(i + 1) * 128],
                                                                         rhs=gkw_s[d][:, hh * 128:(hh + 1) * 128], start=True, stop=False),
                         [lrT[d], gkw_s[d]], [ps])
                    P.op("pe", C.matmul(ps[:, 0:128], lhsT=ones[0:1, :], rhs=gkb_s[d][:, hh * 128:(hh + 1) * 128],
                                                                    start=False, stop=True), [ones, gkb_s[d]], [ps])
                    P.op("act", C.activation(out=tmpf[:, 0:128], in_=ps[:, 0:128], func=AF.Exp, scale=-1.0), [ps], [tmpf])
                    P.op("act", C.activation(out=tmpf2[:, 0:128], in_=tmpf[:, 0:128], func=AF.Ln, bias=oneb[:, 0:1], scale=1.0), [tmpf, oneb], [tmpf2])
                    P.op("dve", C.tensor_scalar(out=gtok[d][:, i, :], in0=tmpf2[:, 0:128], scalar1=-1.0 / 16.0, scalar2=None, op0=ALU.mult),
                         [tmpf2], [gtok[d]])
            wload(hh + 1, False)
            scan_head(256, [kT[0], kT[0]], [ktok[0], ktok[0]], gtok)
            finalize_head(256, wD, glang_s, hh * 2)
            wload(hh + 1, True)
            if stop_after == "l0c":
                return
        for hh in range(8):
            wf = [wB, wE]
            for blk in BLOCKS:
                N = blk[1] - blk[0]
                ps = next_pp()
                proj_feat(wA, 0, 128, blk, ps)
                P.op("act", C.copy(out=qT[:, blk[0]:blk[1]], in_=ps[:, 0:N]), [ps], [qT])
                for d in range(2):
                    ps = next_pp()
                    proj_feat(wC, 128 * d, 128, blk, ps)
                    P.op("act", C.activation(out=tmpf[:, 0:N], in_=ps[:, 0:N], func=AF.Sigmoid, scale=-1.0), [ps], [tmpf])
                    P.op("dve", C.tensor_scalar(out=kT[d][:, blk[0]:blk[1]], in0=tmpf[:, 0:N], scalar1=omlT[:, d, hh:hh + 1],
                                                                                 scalar2=None, op0=ALU.mult), [tmpf, omlT], [kT[d]])
            for i in range(NTILE):
                ps = next_pp()
                proj_tok(wB, 0, 128, i, ps)
                P.op("act", C.copy(out=vtok[:, i, 0:128], in_=ps[:, 0:128]), [ps], [vtok])
                psf = next_pp()
                proj_tok(wC, 0, 256, i, psf)
                P.op("act", C.activation(out=tmpf[:, 0:256], in_=psf[:, 0:256], func=AF.Sigmoid), [psf], [tmpf])
                P.op("dve", C.tensor_tensor(out=tmpf2[:, 0:256].rearrange("p (d c) -> p d c", d=2), in0=tmpf[:, 0:256].rearrange("p (d c) -> p d c", d=2),
                                            in1=oml_bc[:, :, hh * 128:(hh + 1) * 128], op=ALU.mult), [tmpf, oml_bc], [tmpf2])
                for d in range(2):
                    P.op("pool", C.tensor_tensor(out=ktok[d][:, i, :], in0=oml_bc[:, d, hh * 128:(hh + 1) * 128], in1=tmpf2[:, d * 128:(d + 1) * 128], op=ALU.subtract),
                         [tmpf2, oml_bc], [ktok[d]])
                    P.op("dve", C.tensor_tensor(out=gtok[d][:, i, :], in0=tmpf2[:, d * 128:(d + 1) * 128], in1=lb_bc[:, d, hh * 128:(hh + 1) * 128], op=ALU.add),
                         [tmpf2, lb_bc], [gtok[d]])
            for d in range(2):
                P.op("act", C.activation(out=gtok[d][:], in_=gtok[d][:], func=AF.Ln), [gtok[d]], [gtok[d]])
            wload(4 + hh + 1, False)
            scan_head(128, kT, ktok, gtok)
            finalize_head(128, wD, hgng_s, 8 + hh)
            wload(4 + hh + 1, True)
        P.barrier()
        out_proj_residual(ev_w_out, xT, x1T, b, 16)
        P.barrier()


    def carve(name, ap):
        return T(P, name, ap, "sbuf")
    I32 = mybir.dt.int32
    gt0f = gtok[0][:].rearrange("p a b -> p (a b)")
    gt1f = gtok[1][:].rearrange("p a b -> p (a b)")
    kt0f = ktok[0][:].rearrange("p a b -> p (a b)")
    kt1f = ktok[1][:].rearrange("p a b -> p (a b)")
    CH = 1152
    s_Dre = carve("s_Dre", ws[:, 0:CH]); s_Dim = carve("s_Dim", ws[:, CH:2 * CH])
    s_t1 = carve("s_t1", ws[:, 2 * CH:3 * CH]); s_t2 = carve("s_t2", ws[:, 3 * CH:4 * CH])
    s_c = carve("s_c", gt0f[:, 0:CH]); s_s = carve("s_s", gt0f[:, CH:2 * CH])
    s_y = carve("s_y", gt1f)
    s_ub = carve("s_ub", qT[:])
    s_Xre = carve("s_Xre", kt0f[:, 0:CH]); s_Xim = carve("s_Xim", kt0f[:, CH:2 * CH])
    s_ki = P.sbuf("s_ki", [128, CH], I32)
    s_tt = P.sbuf("s_tt", [128, CH], F32)
    s_mats = [P.sbuf("s_mats%d" % i, [128, 512], BF16) for i in range(2)]
    thn = P.sbuf("thn", [128, 64], F32)
    rho = P.sbuf("rho", [128, 64], F32)
    offB = P.sbuf("offB", [128, 64], F32)
    dsk = P.sbuf("dsk", [128, 8], F32)
    esk = P.sbuf("esk", [128, 8], F32)
    maskP = P.sbuf("maskP_s", [128, 128], BF16)
    maskN = P.sbuf("maskN_s", [128, 128], BF16)
    onesK = P.sbuf("onesK", [128, 64], BF16)
    fng = P.sbuf("fng", [128, 8], F32)
    zero_c = P.sbuf("zero_c", [128, 1], F32)
    hpi = P.sbuf("hpi", [128, 1], F32)
    q25 = P.sbuf("q25", [128, 1], F32)
    a_kT = carve("a_kT", kT[0][0:64, :])
    a_qr = carve("a_qr", kT[1][0:64, 0:2048])
    a_vt = carve("a_vt", ktok[0][:])
    onesK2 = carve("onesK2", Sbf[1][:, 0, 0:128])
    a_sg = carve("a_sg", vtok[:, 0:16, :])
    a_cos = carve("a_cos", gt0f[0:64, 0:2048])
    a_sin = carve("a_sin", gt1f[0:64, 0:2048])
    PT = [carve("PT%d" % i, kt1f[:, 512 * i:512 * (i + 1)]) for i in range(4)] + \
         [carve("PT%d" % (4 + i), qT[:, 512 * i:512 * (i + 1)]) for i in range(4)]
    zblk = carve("zblk", vtok[:].rearrange("p a b -> p (a b)")[:, 0:4096].rearrange("p (k n) -> p k n", k=8))
    LATB = [(256, 768), (768, 1280), (1280, 1792), (1792, 2304)]
    S5B = [(0, 256), (256, 768), (768, 1152), (1152, 1408), (1408, 1920), (1920, 2304)]

    def s5_prep():
        lre = carve("p_lre", ws[:, 0:64]); lim = carve("p_lim", ws[:, 64:128]); ldt = carve("p_ldt", ws[:, 128:192])
        ar = carve("p_ar", ws[:, 192:256]); ai = carve("p_ai", ws[:, 256:320]); den = carve("p_den", ws[:, 320:384])
        cr = carve("p_cr", ws[:, 384:448]); ci = carve("p_ci", ws[:, 448:512]); tA = carve("p_tA", ws[:, 512:576]); tB = carve("p_tB", ws[:, 576:640])
        lamre = carve("p_lamre", ws[:, 640:704]); lamim = carve("p_lamim", ws[:, 704:768])
        kiI = s_ki
        Bre = carve("p_Bre", ws[:, 1024:1536].rearrange("p (a b) -> p a b", b=16)); Bim = carve("p_Bim", ws[:, 1536:2048].rearrange("p (a b) -> p a b", b=16))
        Cre = carve("p_Cre", ws[:, 2048:2560].rearrange("p (a b) -> p a b", b=16)); Cim = carve("p_Cim", ws[:, 2560:3072].rearrange("p (a b) -> p a b", b=16))
        Bbr = carve("p_Bbr", ws[:, 3072:3584].rearrange("p (a b) -> p a b", b=16)); Bbi = carve("p_Bbi", ws[:, 3584:4096].rearrange("p (a b) -> p a b", b=16))
        tC = carve("p_tC", ws[:, 4096:4608].rearrange("p (a b) -> p a b", b=16))
        inT = carve("p_inT", gt0f[:, 0:128]); stg = carve("p_stg", gt0f[:, 128:640]); stgb = s_mats[0]
        idn = carve("p_idn", gt0f[:, 640:768])
        P.dma("sp", lamre, lamre[:], lamTre, lamTre[:]); P.dma("sp", lamim, lamim[:], lamTim, lamTim[:]); P.dma("sp", ldt, ldt[:], ldtT, ldtT[:])
        for t_, d_ in ((Bre, BTre), (Bim, BTim), (Cre, CTre), (Cim, CTim)):
            P.dma("sp", t_, t_[:], d_, d_[:])
        P.dma("sp", dsk, dsk[:], dskT, dskT[:]); P.dma("sp", s_tt, s_tt[:], iota_d, iota_d[:])
        P.op("pool", C.memset(zero_c[:], 0.0), [], [zero_c]); P.op("pool", C.memset(hpi[:], float(np.pi / 2)), [], [hpi]); P.op("pool", C.memset(q25[:], 0.25), [], [q25])
        P.op("pool", C.memset(idn[:], 0.0), [], [idn])
        P.op("pool", C.affine_select(out=idn[:], in_=idn[:], pattern=[[-1, 128]], compare_op=ALU.not_equal, fill=1.0, base=0, channel_multiplier=1), [idn], [idn])
        P.op("act", C.activation(out=ldt[:], in_=ldt[:], func=AF.Exp), [ldt], [ldt])
        P.op("dve", C.tensor_tensor(out=lre[:], in0=lamre[:], in1=ldt[:], op=ALU.mult), [lamre, ldt], [lre])
        P.op("dve", C.tensor_tensor(out=lim[:], in0=lamim[:], in1=ldt[:], op=ALU.mult), [lamim, ldt], [lim])
        P.op("act", C.activation(out=rho[:], in_=lre[:], func=AF.Exp), [lre], [rho])
        P.op("dve", C.tensor_scalar(out=thn[:], in0=lim[:], scalar1=float(1.0 / (2 * np.pi)), scalar2=None, op0=ALU.mult), [lim], [thn])
        P.op("dve", C.tensor_scalar(out=offB[:], in0=thn[:], scalar1=float(CH), scalar2=None, op0=ALU.mult), [thn], [offB])

        def sincos(outs_, outc_, turns, tmp):
            P.op("dve", C.tensor_copy(out=kiI[:, 0:64], in_=turns[:]), [turns], [kiI])
            P.op("dve", C.tensor_copy(out=tmp[:], in_=kiI[:, 0:64]), [kiI], [tmp])
            P.op("dve", C.tensor_tensor(out=tmp[:], in0=turns[:], in1=tmp[:], op=ALU.subtract), [turns, tmp], [tmp])
            P.op("act", C.activation(out=outs_[:], in_=tmp[:], func=AF.Sin, scale=float(2 * np.pi), bias=zero_c[:, 0:1]), [tmp, zero_c], [outs_])
            P.op("dve", C.tensor_scalar(out=turns[:], in0=turns[:], scalar1=0.25, scalar2=None, op0=ALU.add), [turns], [turns])
            P.op("dve", C.tensor_copy(out=kiI[:, 0:64], in_=turns[:]), [turns], [kiI])
            P.op("dve", C.tensor_copy(out=tmp[:], in_=kiI[:, 0:64]), [kiI], [tmp])
            P.op("dve", C.tensor_tensor(out=tmp[:], in0=turns[:], in1=tmp[:], op=ALU.subtract), [turns, tmp], [tmp])
            P.op("act", C.activation(out=outc_[:], in_=tmp[:], func=AF.Sin, scale=float(2 * np.pi), bias=zero_c[:, 0:1]), [tmp, zero_c], [outc_])
        P.op("dve", C.tensor_copy(out=tA[:], in_=thn[:]), [thn], [tA])
        sincos(ai, ar, tA, tB)
        P.op("dve", C.tensor_tensor(out=ar[:], in0=ar[:], in1=rho[:], op=ALU.mult), [ar, rho], [ar])
        P.op("dve", C.tensor_tensor(out=ai[:], in0=ai[:], in1=rho[:], op=ALU.mult), [ai, rho], [ai])
        P.op("dve", C.tensor_scalar(out=ar[:], in0=ar[:], scalar1=-1.0, scalar2=None, op0=ALU.add), [ar], [ar])
        P.op("dve", C.tensor_tensor(out=den[:], in0=lamre[:], in1=lamre[:], op=ALU.mult), [lamre], [den])
        P.op("dve", C.tensor_tensor(out=tA[:], in0=lamim[:], in1=lamim[:], op=ALU.mult), [lamim], [tA])
        P.op("dve", C.tensor_tensor(out=den[:], in0=den[:], in1=tA[:], op=ALU.add), [den, tA], [den])
        P.op("dve", C.reciprocal(out=den[:], in_=den[:]), [den], [den])
        P.op("dve", C.tensor_tensor(out=cr[:], in0=ar[:], in1=lamre[:], op=ALU.mult), [ar, lamre], [cr])
        P.op("dve", C.tensor_tensor(out=tA[:], in0=ai[:], in1=lamim[:], op=ALU.mult), [ai, lamim], [tA])
        P.op("dve", C.tensor_tensor(out=cr[:], in0=cr[:], in1=tA[:], op=ALU.add), [cr, tA], [cr])
        P.op("dve", C.tensor_tensor(out=cr[:], in0=cr[:], in1=den[:], op=ALU.mult), [cr, den], [cr])
        P.op("dve", C.tensor_tensor(out=ci[:], in0=ai[:], in1=lamre[:], op=ALU.mult), [ai, lamre], [ci])
        P.op("dve", C.tensor_tensor(out=tA[:], in0=ar[:], in1=lamim[:], op=ALU.mult), [ar, lamim], [tA])
        P.op("dve", C.tensor_tensor(out=ci[:], in0=ci[:], in1=tA[:], op=ALU.subtract), [ci, tA], [ci])
        P.op("dve", C.tensor_tensor(out=ci[:], in0=ci[:], in1=den[:], op=ALU.mult), [ci, den], [ci])
        for dr in range(2):
            crb = cr[:, dr * 32:(dr + 1) * 32].unsqueeze(2).to_broadcast([128, 32, 16])
            cib = ci[:, dr * 32:(dr + 1) * 32].unsqueeze(2).to_broadcast([128, 32, 16])
            P.op("dve", C.tensor_tensor(out=Bbr[:], in0=Bre[:], in1=crb, op=ALU.mult), [Bre, cr], [Bbr])
            P.op("dve", C.tensor_tensor(out=tC[:], in0=Bim[:], in1=cib, op=ALU.mult), [Bim, ci], [tC])
            P.op("dve", C.tensor_tensor(out=Bbr[:], in0=Bbr[:], in1=tC[:], op=ALU.subtract), [Bbr, tC], [Bbr])
            P.op("dve", C.tensor_tensor(out=Bbi[:], in0=Bim[:], in1=crb, op=ALU.mult), [Bim, cr], [Bbi])
            P.op("dve", C.tensor_tensor(out=tC[:], in0=Bre[:], in1=cib, op=ALU.mult), [Bre, ci], [tC])
            P.op("dve", C.tensor_tensor(out=Bbi[:], in0=Bbi[:], in1=tC[:], op=ALU.add), [Bbi, tC], [Bbi])
            for pr in range(32):
                c0 = 32 * (pr % 4)
                for wi, src in enumerate((Bbr, Bbi)):
                    P.op("pool", C.memset(inT[:], 0.0), [], [inT])
                    P.op("dve", C.tensor_copy(out=inT[0:64, c0:c0 + 16], in_=src[0:64, pr, :]), [src], [inT])
                    P.op("dve", C.tensor_copy(out=inT[64:128, c0 + 16:c0 + 32], in_=src[64:128, pr, :]), [src], [inT])
                    ps = next_pp()
                    P.op("pe", C.transpose(ps[:, 0:128], inT[:], idn[:]), [inT, idn], [ps])
                    P.op("act", C.copy(out=stgb[:, wi * 128:(wi + 1) * 128], in_=ps[:, 0:128]), [ps], [stgb])
                P.op("pool", C.memset(stg[:, 0:256], 0.0), [], [stg])
                P.op("dve", C.tensor_copy(out=stg[0:64, c0:c0 + 16], in_=Cre[0:64, pr, :]), [Cre], [stg])
                P.op("dve", C.tensor_copy(out=stg[64:128, c0 + 16:c0 + 32], in_=Cre[64:128, pr, :]), [Cre], [stg])
                P.op("dve", C.tensor_scalar(out=stg[0:64, 128 + c0:128 + c0 + 16], in0=Cim[0:64, pr, :], scalar1=-1.0, scalar2=None, op0=ALU.mult), [Cim], [stg])
                P.op("dve", C.tensor_scalar(out=stg[64:128, 128 + c0 + 16:128 + c0 + 32], in0=Cim[64:128, pr, :], scalar1=-1.0, scalar2=None, op0=ALU.mult), [Cim], [stg])
                P.op("act", C.copy(out=stgb[:, 256:512], in_=stg[:, 0:256]), [stg], [stgb])
                P.dma("sp", mats_d, mats_d[dr * 32 + pr], stgb, stgb[:])

    s_tabs = [(s_c, s_s), (P.sbuf("s_c1", [128, CH], F32), P.sbuf("s_s1", [128, CH], F32))]
    s_tA = P.sbuf("s_tA", [128, CH], F32)
    EDz = [carve("EDz%d" % d, s_tA[:, 512 * d:512 * (d + 1)].rearrange("p (j c) -> p j c", j=4)) for d in range(2)]
    negm = P.sbuf("negm", [128, 4], F32)
    P.dma("sp", negm, negm[:], D["c_negm"], D["c_negm"][:])

    def s5_tab_a1(col, chunk, tb):
        if chunk == 0:
            P.op("dve", C.tensor_scalar(out=s_tA[:], in0=s_tt[:], scalar1=thn[:, col:col + 1], scalar2=None, op0=ALU.mult), [s_tt, thn], [s_tA])
        else:
            P.op("dve", C.tensor_scalar(out=s_tA[:], in0=s_tt[:], scalar1=thn[:, col:col + 1], scalar2=offB[:, col:col + 1], op0=ALU.mult, op1=ALU.add),
                 [s_tt, thn, offB], [s_tA])
        P.op("dve", C.tensor_copy(out=s_ki[:], in_=s_tA[:]), [s_tA], [s_ki])

    def s5_tab_a2(tb):
        cc, ss = s_tabs[tb]
        P.op("act", C.copy(out=ss[:], in_=s_ki[:]), [s_ki], [ss])

    def s5_tab_a(col, chunk, tb):
        s5_tab_a1(col, chunk, tb)
        s5_tab_a2(tb)

    def s5_tab_b(tb):
        cc, ss = s_tabs[tb]
        P.op("dve", C.tensor_tensor(out=s_tA[:], in0=s_tA[:], in1=ss[:], op=ALU.subtract), [s_tA, ss], [s_tA])
        P.op("act", C.activation(out=ss[:], in_=s_tA[:], func=AF.Sin, scale=float(2 * np.pi), bias=zero_c[:, 0:1]), [s_tA, zero_c], [ss])
        P.op("act", C.activation(out=s_tA[:], in_=s_tA[:], func=AF.Abs), [s_tA], [s_tA])
        P.op("act", C.activation(out=cc[:], in_=s_tA[:], func=AF.Sin, scale=float(-2 * np.pi), bias=hpi[:, 0:1]), [s_tA, hpi], [cc])

    def s5_layer(b):
        uw = [wA, wB]
        load_w(uw[0], od_w_in, 2560, 128)
        for o in range(8):
            if o + 1 < 8:
                load_w(uw[(o + 1) % 2], od_w_in, 2560 + (o + 1) * 128, 128)
            for blk in BLOCKS:
                N = blk[1] - blk[0]
                ps = next_pp()
                proj_feat(uw[o % 2], 0, 128, blk, ps)
                P.op("act", C.copy(out=s_ub[:, blk[0]:blk[1]], in_=ps[:, 0:N]), [ps], [s_ub])
                P.op("dve", C.tensor_scalar(out=s_y[:, blk[0]:blk[1]], in0=ps[:, 0:N], scalar1=dsk[:, o:o + 1], scalar2=None, op0=ALU.mult), [ps, dsk], [s_y])
            units = [(dr, pq, ch) for dr in range(2) for pq in range(4) for ch in range(2)]
            ybank = {S5B[0]: (banks[0], 0), S5B[3]: (banks[0], 256), S5B[1]: (banks[1], 0), S5B[2]: (banks[2], 0),
                     S5B[4]: (banks[3], 0), S5B[5]: (banks[4], 0)}
            for bi_ in range(5):
                P.op("dve", C.memset(banks[bi_][:], 0.0), [], [banks[bi_]])
            pp_list[0] = [banks[5], banks[6], banks[7]]
            def uctx(ui):
                dr, pq, ch = units[ui]
                col = dr * 32 + o * 4 + pq
                mtt = s_mats[(ui // 2) % 2]
                if dr == 0:
                    chunks = [[S5B[0], S5B[1], S5B[2]], [S5B[3], S5B[4], S5B[5]]]
                else:
                    chunks = [[S5B[0], S5B[4], S5B[5]], [S5B[1], S5B[2], S5B[3]]]

                def tau_slice(blk):
                    c0, c1 = blk
                    if dr == 0:
                        lo = c0 - ch * CH
                        return slice(lo, lo + (c1 - c0))
                    t_hi = (255 - c0) if c1 <= 256 else (2559 - c0)
                    t_lo = (255 - (c1 - 1)) if c1 <= 256 else (2559 - (c1 - 1))
                    hi = t_hi - ch * CH; lo = t_lo - ch * CH
                    return slice(hi, lo - 1 if lo > 0 else None, -1)
                return dr, pq, ch, col, mtt, chunks[ch], tau_slice

            dps = {}

            def Dmm(ui, half):
                dr, pq, ch, col, mtt, blks, tsl = uctx(ui)
                if ch == 0 and half == 0:
                    P.dma("sp", mtt, mtt[:], mats_d, mats_d[col])
                lst = []
                for blk in blks:
                    N = blk[1] - blk[0]
                    ps = next_pp()
                    P.op("pe", C.matmul(ps[:, 0:N], lhsT=mtt[:, half * 128:(half + 1) * 128], rhs=s_ub[:, blk[0]:blk[1]], start=True, stop=True),
                         [mtt, s_ub], [ps])
                    lst.append((blk, ps))
                dps[(ui, half)] = lst

            def Devac(ui, half):
                dr, pq, ch, col, mtt, blks, tsl = uctx(ui)
                dst = s_Dre if half == 0 else s_Dim
                for (blk, ps) in dps.pop((ui, half)):
                    N = blk[1] - blk[0]
                    P.op("act", C.copy(out=dst[:, tsl(blk)], in_=ps[:, 0:N]), [ps], [dst])

            def readout(ui):
                dr, pq, ch, col, mt, blks, tau_slice = uctx(ui)
                for blk in blks:
                    if blk[0] == 0:
                        continue
                    N = blk[1] - blk[0]
                    sl = tau_slice(blk)
                    if dr == 1:
                        P.op("act", C.copy(out=xst[:, 0:N], in_=s_Xre[:, sl]), [s_Xre], [xst])
                        P.op("act", C.copy(out=xst[:, 512:512 + N], in_=s_Xim[:, sl]), [s_Xim], [xst])
                        r_re, r_im, rt = xst[:, 0:N], xst[:, 512:512 + N], [xst]
                    else:
                        r_re, r_im, rt = s_Xre[:, sl], s_Xim[:, sl], [s_Xre, s_Xim]
                    yb, yo = ybank[blk]
                    P.op("pe", C.matmul(yb[:, yo:yo + N], lhsT=mt[:, 256:384], rhs=r_re, start=False, stop=False, skip_group_check=True), [mt] + rt, [yb])
                    P.op("pe", C.matmul(yb[:, yo:yo + N], lhsT=mt[:, 384:512], rhs=r_im, start=False, stop=False, skip_group_check=True), [mt] + rt, [yb])

            pend_ro = []
            s5_tab_a(units[0][0] * 32 + o * 4 + units[0][1], units[0][2], 0)
            s5_tab_b(0)
            Dmm(0, 0)
            for ui in range(len(units)):
                dr, pq, ch, col, mt, blks, tau_slice = uctx(ui)
                cc, ss = s_tabs[ui % 2]
                nxt = units[ui + 1] if ui + 1 < len(units) else None
                rho_b = rho[:, col:col + 1].to_broadcast([128, CH])
                Devac(ui, 0)
                Dmm(ui, 1)
                Devac(ui, 1)
                if nxt is not None:
                    ncol = nxt[0] * 32 + o * 4 + nxt[1]
                    P.op("act", C.activation(out=s_tA[:], in_=s_tt[:], func=AF.Identity, scale=thn[:, ncol:ncol + 1],
                                             bias=(zero_c[:, 0:1] if nxt[2] == 0 else offB[:, ncol:ncol + 1])), [s_tt, thn, offB, zero_c], [s_tA])
                    P.op("act", C.copy(out=s_ki[:], in_=s_tA[:]), [s_tA], [s_ki])
                    s5_tab_a2((ui + 1) % 2)
                if pend_ro:
                    readout(pend_ro.pop(0))
                P.op("dve", C.tensor_tensor(out=s_t2[:], in0=ss[:], in1=s_Dre[:], op=ALU.mult), [ss, s_Dre], [s_t2])
                P.op("dve", C.tensor_tensor(out=s_Dre[:], in0=cc[:], in1=s_Dre[:], op=ALU.mult), [cc, s_Dre], [s_Dre])
                P.op("dve", C.tensor_tensor(out=s_t1[:], in0=ss[:], in1=s_Dim[:], op=ALU.mult), [ss, s_Dim], [s_t1])
                P.op("dve", C.tensor_tensor(out=s_Dim[:], in0=cc[:], in1=s_Dim[:], op=ALU.mult), [cc, s_Dim], [s_Dim])
                P.op("dve", C.tensor_tensor(out=s_Dre[:], in0=s_Dre[:], in1=s_t1[:], op=ALU.add), [s_Dre, s_t1], [s_Dre])
                P.op("dve", C.tensor_tensor(out=s_Dim[:], in0=s_Dim[:], in1=s_t2[:], op=ALU.subtract), [s_Dim, s_t2], [s_Dim])
                if nxt is not None:
                    s5_tab_b((ui + 1) % 2)
                for (src, dst, zi) in ((s_Dre, s_t1, 0), (s_Dim, s_t2, 1)):
                    init = 0.0 if ch == 0 else zst[:, zi:zi + 1]
                    rd = [src, rho] + ([] if ch == 0 else [zst])
                    P.op("dve", C.tensor_tensor_scan(out=dst[:], data0=rho_b, data1=src[:], initial=init, op0=ALU.mult, op1=ALU.add), rd, [dst])
                if ch == 0:
                    P.op("dve", C.tensor_copy(out=zst[:, 0:1], in_=s_t1[:, CH - 1:CH]), [s_t1], [zst])
                    P.op("dve", C.tensor_copy(out=zst[:, 1:2], in_=s_t2[:, CH - 1:CH]), [s_t2], [zst])
                if nxt is not None:
                    Dmm(ui + 1, 0)
                bl = slice(256, CH) if ch == 0 else slice(0, CH)
                P.op("dve", C.tensor_tensor(out=s_Dre[:, bl], in0=ss[:, bl], in1=s_t2[:, bl], op=ALU.mult), [ss, s_t2], [s_Dre])
                P.op("dve", C.tensor_tensor(out=s_Dim[:, bl], in0=ss[:, bl], in1=s_t1[:, bl], op=ALU.mult), [ss, s_t1], [s_Dim])
                P.op("dve", C.tensor_tensor(out=s_t1[:, bl], in0=cc[:, bl], in1=s_t1[:, bl], op=ALU.mult), [cc, s_t1], [s_t1])
                P.op("dve", C.tensor_tensor(out=s_Xre[:, bl], in0=s_t1[:, bl], in1=s_Dre[:, bl], op=ALU.subtract), [s_t1, s_Dre], [s_Xre])
                P.op("dve", C.tensor_tensor(out=s_t2[:, bl], in0=cc[:, bl], in1=s_t2[:, bl], op=ALU.mult), [cc, s_t2], [s_t2])
                P.op("dve", C.tensor_tensor(out=s_Xim[:, bl], in0=s_t2[:, bl], in1=s_Dim[:, bl], op=ALU.add), [s_t2, s_Dim], [s_Xim])
                pend_ro.append(ui)
            readout(pend_ro.pop(0))
            pp_list[0] = ps_p
            for blk in S5B:
                N = blk[1] - blk[0]
                yb, yo = ybank[blk]
                P.op("dve", C.tensor_tensor(out=s_y[:, blk[0]:blk[1]], in0=yb[:, yo:yo + N], in1=s_y[:, blk[0]:blk[1]], op=ALU.add), [yb, s_y], [s_y])
                P.op("act", C.activation(out=astage[:, 0:N], in_=s_y[:, blk[0]:blk[1]], func=AF.Gelu), [s_y], [astage])
                P.dma("sp", zT_d, zT_d[o, :, blk[0]:blk[1]], astage, astage[:, 0:N])

    zst = P.sbuf("zst", [128, 2], F32)
    xst = carve("xst", Sbf[0][:].rearrange("p a b -> p (a b)"))

    def attn_layer(b):
        P.op("pool", C.memset(onesK2[:], 1.0), [], [onesK2])
        P.dma("sp", a_cos, a_cos[:], ropeC, ropeC[:])
        P.dma("sp", a_sin, a_sin[:], ropeS, ropeS[:])
        pti = [0]
        sbank = [0]
        for kvh in range(4):
            load_w(wA, od_w_in, 1024 + kvh * 64, 64, dcol=0)
            load_w(wA, od_w_sw, 1024 + kvh * 64, 64, dcol=64)
            load_w(wB, od_w_in, 1280 + kvh * 64, 64)
            load_w(wC, od_w_in, kvh * 256, 256)
            load_w(wD, od_w_sw, kvh * 256, 256)
            for blk in BLOCKS:
                N = blk[1] - blk[0]
                ps = next_pp()
                proj_feat(wA, 0, 64, blk, ps)
                if blk[0] == 0:
                    P.op("act", C.copy(out=a_kT[:, 0:256], in_=ps[0:64, 0:N]), [ps], [a_kT])
                    continue
                lc = slice(blk[0] - 256, blk[1] - 256)
                P.op("dve", C.tensor_tensor(out=tmpf[0:64, 0:N], in0=ps[0:64, 0:N], in1=a_cos[:, lc], op=ALU.mult), [ps, a_cos], [tmpf])
                ps2 = next_pp()
                proj_feat(wA, 64, 64, blk, ps2)
                P.op("dve", C.tensor_tensor(out=tmpf2[0:64, 0:N], in0=ps2[0:64, 0:N], in1=a_sin[:, lc], op=ALU.mult), [ps2, a_sin], [tmpf2])
                P.op("pool", C.tensor_tensor(out=a_kT[:, blk[0]:blk[1]], in0=tmpf[0:64, 0:N], in1=tmpf2[0:64, 0:N], op=ALU.add), [tmpf, tmpf2], [a_kT])
            for i in range(NTILE):
                ps = next_pp()
                proj_tok(wB, 0, 64, i, ps)
                P.op("act", C.copy(out=a_vt[:, i, 0:64], in_=ps[:, 0:64]), [ps], [a_vt])
                P.op("act", C.copy(out=a_vt[:, i, 64:128], in_=ps[:, 0:64]), [ps], [a_vt])
            for gh in range(2):
                load_w(wE, od_w_in, 1536 + kvh * 256 + gh * 128, 128)
                for bi, blk in enumerate(LATB):
                    ps = next_pp()
                    proj_feat(wE, 0, 128, blk, ps)
                    P.op("act", C.activation(out=a_sg[:, 4 * bi:4 * bi + 4, gh * 128:(gh + 1) * 128],
                                             in_=ps[:, 0:512].rearrange("p (a b) -> p a b", b=128), func=AF.Silu), [ps], [a_sg])
            for bi, blk in enumerate(LATB):
                lc = slice(blk[0] - 256, blk[1] - 256)
                qv = a_qr[:].rearrange("p (a g q) -> p a g q", a=4, g=4)
                for g in range(4):
                    ps = next_pp()
                    proj_feat(wC, g * 64, 64, blk, ps)
                    P.op("dve", C.scalar_tensor_tensor(out=tmpf[0:64, 0:512], in0=ps[0:64, 0:512], scalar=0.125, in1=a_cos[:, lc], op0=ALU.mult, op1=ALU.mult),
                         [ps, a_cos], [tmpf])
                    ps2 = next_pp()
                    proj_feat(wD, g * 64, 64, blk, ps2)
                    P.op("dve", C.scalar_tensor_tensor(out=tmpf2[0:64, 0:512], in0=ps2[0:64, 0:512], scalar=0.125, in1=a_sin[:, lc], op0=ALU.mult, op1=ALU.mult),
                         [ps2, a_sin], [tmpf2])
                    P.op("pool", C.tensor_tensor(out=qv[:, :, g, :], in0=tmpf[0:64, 0:512].rearrange("p (a q) -> p a q", a=4),
                                                 in1=tmpf2[0:64, 0:512].rearrange("p (a q) -> p a q", a=4), op=ALU.add), [tmpf, tmpf2], [a_qr])
                for qbl in range(4):
                    qb = 4 * bi + qbl
                    tiles = [(0, None), (1, None)]
                    if qb > 0:
                        tiles.append((2 + qb - 1, maskP))
                    tiles.append((2 + qb, None))
                    if qb < 15:
                        tiles.append((2 + qb + 1, maskN))
                    pts = []
                    for (ti, msk) in tiles:
                        pss = banks[sbank[0] % 4]; sbank[0] += 1
                        pt = PT[pti[0] % 8]; pti[0] += 1
                        P.op("pe", C.matmul(pss[:, 0:512], lhsT=a_kT[:, ti * 128:(ti + 1) * 128], rhs=a_qr[:, qbl * 512:(qbl + 1) * 512], start=True, stop=True),
                             [a_kT, a_qr], [pss])
                        P.op("act", C.activation(out=pt[:], in_=pss[:, 0:512], func=AF.Exp), [pss], [pt])
                        if msk is not None:
                            P.op("dve", C.tensor_tensor(out=pt[:].rearrange("p (g q) -> p g q", g=4), in0=pt[:].rearrange("p (g q) -> p g q", g=4),
                                                        in1=msk[:].unsqueeze(1).to_broadcast([128, 4, 128]), op=ALU.mult), [pt, msk], [pt])
                        pts.append((ti, pt))
                    psA, psB = banks[4], banks[5]
                    for n_, (ti, pt) in enumerate(pts):
                        P.op("pe", C.matmul(psA[:, 0:512], lhsT=a_vt[:, ti, :], rhs=pt[:], start=(n_ == 0), stop=(n_ == len(pts) - 1)),
                             [a_vt, pt], [psA])
                    for n_, (ti, pt) in enumerate(pts):
                        P.op("pe", C.matmul(psB[:, 0:512], lhsT=onesK2[:], rhs=pt[:], start=(n_ == 0), stop=(n_ == len(pts) - 1)),
                             [onesK2, pt], [psB])
                    for h_ in range(2):
                        rows = slice(64 * h_, 64 * h_ + 64)
                        for gh in range(2):
                            cb = 2 * gh + h_
                            P.op("dve", C.tensor_scalar(out=tmpf[rows, gh * 128:(gh + 1) * 128], in0=psB[rows, cb * 128:(cb + 1) * 128],
                                                        scalar1=esk[rows, kvh * 2 + gh:kvh * 2 + gh + 1], scalar2=None, op0=ALU.add), [psB, esk], [tmpf])
                    P.op("dve", C.reciprocal(out=tmpf[:, 0:256], in_=tmpf[:, 0:256]), [tmpf], [tmpf])
                    for h_ in range(2):
                        rows = slice(64 * h_, 64 * h_ + 64)
                        P.op("dve", C.tensor_tensor(out=tmpf2[rows, 0:256].rearrange("p (a q) -> p a q", a=2),
                                                    in0=psA[rows, 0:512].rearrange("p (g q) -> p g q", g=4)[:, h_::2, :],
                                                    in1=tmpf[rows, 0:256].rearrange("p (a q) -> p a q", a=2), op=ALU.mult), [psA, tmpf], [tmpf2])
                    P.op("pool", C.tensor_tensor(out=astage[:, 0:256], in0=tmpf2[:, 0:256], in1=a_sg[:, qb, :], op=ALU.mult), [tmpf2, a_sg], [astage])
                    for gh in range(2):
                        P.dma("sp", aT_d, aT_d[kvh * 2 + gh, :, 256 + qb * 128:256 + (qb + 1) * 128], astage, astage[:, gh * 128:(gh + 1) * 128])

    zres = [carve("zres%d" % i, ap_) for i, ap_ in enumerate([
        qT[:, 0:2048], kT[0][:, 0:2048], kT[1][:, 0:2048],
        ktok[0][:].rearrange("p a b -> p (a b)")[:, 0:2048], ktok[1][:].rearrange("p a b -> p (a b)")[:, 0:2048],
        vtok[:].rearrange("p a b -> p (a b)")[:, 0:2048],
        wC[:].rearrange("p a b -> p (a b)"), wD[:].rearrange("p a b -> p (a b)")])]

    def glu_layer(b):
        for kt in range(8):
            P.dma("sp", zres[kt], zres[kt][:], zT_d, zT_d[kt, :, 256:2304])

        def zproj(wt, blk, out_ps):
            c0, c1 = blk[0] - 256, blk[1] - 256
            for kt in range(8):
                P.op("pe", C.matmul(out_ps[:, 0:512], lhsT=wt[:, kt, 0:128], rhs=zres[kt][:, c0:c1], start=(kt == 0), stop=(kt == 7)),
                     [wt, zres[kt]], [out_ps])
        for jt in range(8):
            load_w(wA, glu_w, jt * 128, 128)
            load_w(wB, glu_w, 1024 + jt * 128, 128)
            load_w(wE, od_w_in, 3584 + jt * 128, 128)
            for blk in LATB:
                psa = next_pp()
                zproj(wA, blk, psa)
                psb = next_pp()
                zproj(wB, blk, psb)
                psg = next_pp()
                proj_feat(wE, 0, 128, blk, psg)
                P.op("act", C.activation(out=tmpf[:], in_=psb[:, 0:512], func=AF.Sigmoid), [psb], [tmpf])
                P.op("act", C.activation(out=sgate[:], in_=psg[:, 0:512], func=AF.Sigmoid), [psg], [sgate])
                P.op("dve", C.tensor_tensor(out=tmpf2[:], in0=psa[:, 0:512], in1=tmpf[:], op=ALU.mult), [psa, tmpf], [tmpf2])
                P.op("dve", C.tensor_tensor(out=sgate[:], in0=psg[:, 0:512], in1=sgate[:], op=ALU.mult), [psg, sgate], [sgate])
                P.op("dve", C.tensor_tensor(out=astage[:], in0=tmpf2[:], in1=sgate[:], op=ALU.mult), [tmpf2, sgate], [astage])
                P.dma("sp", aT_d, aT_d[8 + jt, :, blk[0]:blk[1]], astage, astage[:])

    fin_t = []

    def final_layer(b):
        if not fin_t:
            fin_t.append(T(P, "f_x2", ws[:, 0:4096].rearrange("p (k n) -> p k n", k=8), "sbuf"))
            fin_t.append(T(P, "f_sq0", gt0f[:, 0:2048].rearrange("p (k n) -> p k n", k=4), "sbuf"))
            fin_t.append(T(P, "f_sq1", gt1f[:, 0:2048].rearrange("p (k n) -> p k n", k=4), "sbuf"))
        x2, sq0, sq1 = fin_t
        load_w(wout, od_w_out, 0, 1024, 0, 16)
        for (c0, c1) in LATB:
            N = 512
            for kt in range(16):
                P.dma("sp", akt[kt], akt[kt][:, 0:N], aT_d, aT_d[kt, :, c0:c1])
            for jt in range(8):
                ps = next_pp()
                for kt in range(16):
                    P.op("pe", C.matmul(ps[:, 0:N], lhsT=wout[:, kt, jt * 128:(jt + 1) * 128], rhs=akt[kt][:, 0:N],
                                        start=(kt == 0), stop=(kt == 15)), [wout, akt[kt]], [ps])
                P.dma("sp", xres, xres[:, 0:N], x1T, x1T[b, jt, :, c0:c1])
                P.op("dve", C.scalar_tensor_tensor(out=x2[:, jt, :], in0=ps[:, 0:N], scalar=mod[:, 16 + jt, b:b + 1],
                                                   in1=xres[:, 0:N], op0=ALU.mult, op1=ALU.add), [ps, mod, xres], [x2])
                sq = sq0 if jt < 4 else sq1
                P.op("act", C.activation(out=sq[:, jt % 4, :], in_=x2[:, jt, :], func=AF.Square), [x2], [sq])
            ps = next_pp()
            for kt in range(8):
                sq = sq0 if kt < 4 else sq1
                P.op("pe", C.matmul(ps[:, 0:N], lhsT=ones[:], rhs=sq[:, kt % 4, :], start=(kt == 0), stop=(kt == 7)), [ones, sq], [ps])
            rstd_from_ps(ps, N, 1024.0)
            for kt in range(8):
                P.op("dve", C.scalar_tensor_tensor(out=xo[:, 0:N], in0=x2[:, kt, :], scalar=fng[:, kt:kt + 1], in1=rstd[:, 0:N],
                                                   op0=ALU.mult, op1=ALU.mult), [x2, fng, rstd], [xo])
                P.dma("sp", outT, outT[b, kt, :, c0 - 256:c1 - 256], xo, xo[:, 0:N])

    def layer1(b):
        norm_mod(x1T, b)
        P.barrier()
        s5_layer(b)
        P.barrier()
        attn_layer(b)
        P.barrier()
        glu_layer(b)
        P.barrier()
        final_layer(b)
        P.barrier()

    for d in range(2):
        P.dma("pool", gkw_s[d], gkw_s[d][:], gkw, gkw[d])
        P.dma("sp", gkb_s[d], gkb_s[d][:], gkb, gkb[d])
    P.dma("sp", glang_s, glang_s[:], glang, glang[:])
    P.dma("sp", hgng_s, hgng_s[:], hgng, hgng[:])
    P.dma("sp", lbtmp, lbtmp[:, 0, :], lbraw, lbraw[0, 0:1, :].partition_broadcast(128))
    P.dma("sp", lbtmp, lbtmp[:, 1, :], lbraw, lbraw[1, 0:1, :].partition_broadcast(128))
    P.dma("sp", oml_bc, oml_bc[:, 0, :], lbraw, lbraw[0, 1:2, :].partition_broadcast(128))
    P.dma("sp", oml_bc, oml_bc[:, 1, :], lbraw, lbraw[1, 1:2, :].partition_broadcast(128))
    P.op("act", C.activation(out=lbtmp[:], in_=lbtmp[:], func=AF.Exp), [lbtmp], [lbtmp])
    P.op("act", C.activation(out=oml_bc[:], in_=oml_bc[:], func=AF.Exp), [oml_bc], [oml_bc])
    P.op("dve", C.tensor_tensor(out=lb_bc[:], in0=lbtmp[:], in1=oml_bc[:], op=ALU.add), [lbtmp, oml_bc], [lb_bc])
    P.op("dve", C.reciprocal(out=lb_bc[:], in_=lb_bc[:]), [lb_bc], [lb_bc])
    P.op("dve", C.tensor_tensor(out=oml_bc[:], in0=oml_bc[:], in1=lb_bc[:], op=ALU.mult), [oml_bc, lb_bc], [oml_bc])
    P.op("dve", C.tensor_tensor(out=lb_bc[:], in0=lbtmp[:], in1=lb_bc[:], op=ALU.mult), [lbtmp, lb_bc], [lb_bc])
    P.dma("sp", lbT, lbT[:], lbrawT, lbrawT[:])
    P.op("act", C.activation(out=lbT[:], in_=lbT[:], func=AF.Exp), [lbT], [lbT])
    P.op("dve", C.tensor_tensor(out=lbTt[:], in0=lbT[:, :, 0, :], in1=lbT[:, :, 1, :], op=ALU.add), [lbT], [lbTt])
    P.op("dve", C.reciprocal(out=lbTt[:], in_=lbTt[:]), [lbTt], [lbTt])
    P.op("dve", C.tensor_tensor(out=omlT[:], in0=lbT[:, :, 1, :], in1=lbTt[:], op=ALU.mult), [lbT, lbTt], [omlT])

    P.dma("pool", maskP, maskP[:], maskP_d, maskP_d[:]); P.dma("pool", maskN, maskN[:], maskN_d, maskN_d[:])
    P.dma("sp", esk, esk[:], sinkT, sinkT[:]); P.dma("sp", fng, fng[:], fngT, fngT[:])
    P.op("act", C.activation(out=esk[:], in_=esk[:], func=AF.Exp), [esk], [esk])
    P.op("pool", C.memset(onesK[:], 1.0), [], [onesK])
    P.barrier()
    s5_prep()
    P.barrier()
    for L in range(2):
        mod, msc = modL[L], mscL[L]
        adaln(L)
        P.barrier()
    outs = [outT]
    if stop_after == 0:
        outs = [x1T, outT]
    for b in range(2):
        mod, msc = modL[0], mscL[0]
        layer0(b)
        if stop_after == 0:
            continue
        mod, msc = modL[1], mscL[1]
        layer1(b)
    P.wait_all("sp", outs)
    P.barrier()
    P.emit()
    st.close()
    return nc, P


def host_inputs(inp, core):
    f = np.float32
    b0 = 2 * core
    out = {}
    xcat = np.concatenate([inp["ctx"][b0:b0 + 2], inp["x"][b0:b0 + 2]], axis=1)
    out["xT"] = np.ascontiguousarray(xcat.transpose(0, 2, 1).reshape(2, 8, 128, NT)).astype(f)
    cv = np.stack([inp["c"][b0], inp["c"][b0 + 1], inp["c_ctx"]], axis=1)
    out["cT"] = np.ascontiguousarray(cv.reshape(8, 128, 3).transpose(1, 0, 2)).astype(f)
    out["ada_w"] = np.ascontiguousarray(inp["ada_w"]).astype(f)
    out["ada_bT"] = np.ascontiguousarray(inp["ada_b"].reshape(2, 24, 128).transpose(0, 2, 1)).astype(f)
    out["normgT"] = np.ascontiguousarray(inp["norm_g"].reshape(2, 8, 128).transpose(0, 2, 1)).astype(f)
    out["fngT"] = np.ascontiguousarray(inp["final_norm_g"].reshape(8, 128).T).astype(f)
    out["ev_w_in"] = np.ascontiguousarray(inp["ev_w_in"][0]).astype(f)
    out["ev_w_out"] = np.ascontiguousarray(inp["ev_w_out"][0]).astype(f)
    out["gkw"] = np.ascontiguousarray(inp["gla_gk_w"][0]).astype(f)
    out["gkb"] = np.ascontiguousarray(inp["gla_gk_b"][0].reshape(2, 1, 512)).astype(f)
    out["glang"] = np.ascontiguousarray(inp["gla_norm_g"][0].reshape(2, 128).T).astype(f)
    out["hgng"] = np.ascontiguousarray(inp["hgrn_norm_g"][0].reshape(128, 1)).astype(f)
    out["lbraw"] = np.ascontiguousarray(inp["hgrn_lb_raw"]).astype(f)
    out["lbrawT"] = np.ascontiguousarray(inp["hgrn_lb_raw"].reshape(2, 2, 8, 128).transpose(3, 0, 1, 2)).astype(f)
    for k, v in host_consts().items():
        out["c_" + k] = v
    w1 = np.asarray(inp["od_w_in"][0], f)
    out["od_w_in"] = np.ascontiguousarray(w1)
    perm = np.arange(1280)
    dd = perm % 32
    perm = np.where(dd < 16, perm + 16, perm - 16)
    out["od_w_sw"] = np.ascontiguousarray(w1[:, perm])
    out["od_w_out"] = np.ascontiguousarray(inp["od_w_out"][0]).astype(f)
    out["glu_w"] = np.ascontiguousarray(inp["s5_glu_w"][0]).astype(f)
    t = np.arange(2048)
    freqs = 10000.0 ** (-np.arange(16, dtype=np.float64) / 16.0)
    d = np.arange(64)
    pos = np.where((d // 32)[:, None] == 0, (t // 64)[None, :], (t % 64)[None, :]).astype(np.float64)
    ang = pos * freqs[d % 16][:, None]
    sgn = np.where((d % 32) < 16, -1.0, 1.0)[:, None]
    out["ropeC"] = np.cos(ang).astype(f)
    out["ropeS"] = (np.sin(ang) * sgn).astype(f)
    sk = np.asarray(inp["attn_sink"][0], f)
    p = np.arange(128)
    st = np.zeros((128, 8), f)
    for kvh in range(4):
        for gh in range(2):
            st[:, kvh * 2 + gh] = sk[kvh * 4 + 2 * gh + p // 64]
    out["sinkT"] = st
    kk = np.arange(128)[:, None]; qq = np.arange(128)[None, :]
    out["maskP"] = (kk >= qq).astype(f)
    out["maskN"] = (kk <= qq).astype(f)
    def lamT(a):
        return np.ascontiguousarray(np.asarray(a, f).reshape(2, 32, 2, 64).transpose(2, 3, 0, 1).reshape(128, 64))
    out["lamTre"] = lamT(inp["s5_lambda_re"][0])
    out["lamTim"] = lamT(inp["s5_lambda_im"][0])
    ld = np.asarray(inp["s5_log_dt"][0], f).reshape(2, 32, 2)
    out["ldtT"] = np.ascontiguousarray(np.repeat(ld.transpose(2, 0, 1).reshape(2, 1, 64), 64, axis=1).reshape(128, 64))
    def bT(a):
        return np.ascontiguousarray(np.asarray(a, f).reshape(32, 2, 64, 16).transpose(1, 2, 0, 3).reshape(128, 32, 16))
    def cTt(a):
        return np.ascontiguousarray(np.asarray(a, f).reshape(32, 2, 16, 64).transpose(1, 3, 0, 2).reshape(128, 32, 16))
    out["BTre"] = bT(inp["s5_b_re"][0]); out["BTim"] = bT(inp["s5_b_im"][0])
    out["CTre"] = cTt(inp["s5_c_re"][0]); out["CTim"] = cTt(inp["s5_c_im"][0])
    out["dskT"] = np.ascontiguousarray(np.asarray(inp["s5_d"][0], f).reshape(8, 128).T)
    out["iota"] = np.ascontiguousarray(np.broadcast_to(np.arange(1152, dtype=f)[None, :], (128, 1152)))
    return out


def kernel(**inp):
    inp = {k: np.asarray(v) for k, v in inp.items()}
    nc, P = build_program()
    in_maps = [host_inputs(inp, c) for c in range(8)]
    res = run_bass_kernel_spmd(nc, in_maps, core_ids=list(range(8)))
    outs = []
    for c in range(8):
        o = res.results[c]["outT"]
        outs.append(o.reshape(2, 1024, 2048).transpose(0, 2, 1))
    return np.ascontiguousarray(np.concatenate(outs, axis=0)).astype(np.float32)
```

```python
import numpy as np
from contextlib import ExitStack
import concourse.bass as bass
import concourse.mybir as mybir
from concourse.bass_utils import run_bass_kernel_spmd

F32 = mybir.dt.float32
BF16 = mybir.dt.bfloat16
AF = mybir.ActivationFunctionType
ALU = mybir.AluOpType
AX = mybir.AxisListType

ENGS = ("pe", "act", "dve", "pool", "sp")
SAME_ENGINE_SYNC = True
NO_SELF_SYNC = ("pe", "act")

NT = 2304
NTILE = 18
BLOCKS = [(0, 256), (256, 768), (768, 1280), (1280, 1792), (1792, 2304)]
EPS = 1e-6


class T:
    def __init__(self, prog, name, ap_src, kind):
        self.prog = prog
        self.name = name
        self.src = ap_src
        self.kind = kind
        self.writer = None
        self.readers = []
        self.sem = None
        self.dma_count = 0
        self.root = self

    def __getitem__(self, idx):
        return self.src[idx]

    def view(self, name, idx):
        t = T(self.prog, name, self.src[idx], self.kind)
        t.root = self.root
        return t


class Prog:
    def __init__(self, nc, stack):
        self.nc = nc
        self.stack = stack
        self.q = {e: [] for e in ENGS}
        self.cnt = {e: 0 for e in ENGS}
        self.waited = {e: {} for e in ENGS}
        self.sems = {}
        self.semval = {}
        for e in ENGS:
            self.sems[e] = stack.enter_context(nc.semaphore("s_" + e))
        self.ninst = 0

    def sbuf(self, name, shape, dtype):
        t = self.stack.enter_context(self.nc.sbuf_tensor(name, list(shape), dtype))
        return T(self, name, t, "sbuf")

    def psum(self, name, shape, dtype=F32):
        t = self.stack.enter_context(self.nc.psum_tensor(name, list(shape), dtype))
        return T(self, name, t, "psum")

    def dram(self, name, shape, dtype, kind="Internal"):
        t = self.nc.dram_tensor(name, list(shape), dtype, kind=kind)
        return T(self, name, t.ap(), "dram")

    def share_sem(self, t, other):
        t.sem = self._tsem(other)
        t.sem_owner = other

    def _tsem(self, t):
        if t.sem is None:
            key = "t_" + t.name
            self.sems[key] = self.stack.enter_context(self.nc.semaphore("d_" + t.name))
            self.semval[key] = 0
            t.sem = key
        return t.sem

    def _deps(self, eng, reads, writes):
        deps = []
        for t in reads:
            if t.writer is not None:
                deps.append(t.writer)
        for t in writes:
            if t.writer is not None:
                deps.append(t.writer)
            deps.extend(t.readers)
        need = {}
        for (k, v) in deps:
            if k == eng and (not SAME_ENGINE_SYNC or eng in NO_SELF_SYNC):
                continue
            if v > need.get(k, 0):
                need[k] = v
        for k, v in need.items():
            if self.waited[eng].get(k, 0) >= v:
                continue
            self.waited[eng][k] = v
            self.q[eng].append(("wait", k, v))

    limit = None
    nlim = 0

    def op(self, eng, fn, reads=(), writes=()):
        if self.limit is not None:
            self.nlim += 1
            if self.nlim > self.limit:
                return 0
        reads = [t.root for t in reads]
        writes = [t.root for t in writes]
        writes = writes + [t for t in reads if t.kind == "psum" and t not in writes]
        self._deps(eng, reads, writes)
        self.cnt[eng] += 1
        self.ninst += 1
        idx = self.cnt[eng]
        self.q[eng].append(("op", fn, eng, 1))
        for t in writes:
            t.writer = (eng, idx)
            t.readers = []
        for t in reads:
            if t not in writes:
                t.readers.append((eng, idx))
        return idx

    def dma(self, eng, out_t, out_ap, in_t, in_ap, **kw):
        out_t = out_t.root
        in_t = in_t.root
        self._deps(eng, [in_t], [out_t])
        key = self._tsem(out_t)
        self.semval[key] += 16
        val = self.semval[key]
        self.ninst += 1
        self.q[eng].append(("dma", (lambda e: e.dma_start(out=out_ap, in_=in_ap, **kw)), key, 16))
        out_t.writer = (key, val)
        out_t.readers = []
        in_t.readers.append((key, val))

    def wait_all(self, eng, tiles):
        self._deps(eng, [t.root for t in tiles], [])

    def barrier(self):
        for e in ENGS:
            for k in list(self.sems.keys()):
                v = self.cnt[k] if k in self.cnt else self.semval[k]
                if v == 0 or (k == e and (not SAME_ENGINE_SYNC or e in NO_SELF_SYNC)):
                    continue
                if self.waited[e].get(k, 0) >= v:
                    continue
                self.waited[e][k] = v
                self.q[e].append(("wait", k, v))

    def emit(self):
        nc = self.nc
        handles = {"pe": "tensor", "act": "scalar", "dve": "vector", "pool": "gpsimd", "sp": "sync"}
        with nc.Block() as block:
            for e in ENGS:
                items = self.q[e]
                if not items:
                    continue

                def body(engh, items=items):
                    for it in items:
                        if it[0] == "wait":
                            engh.wait_ge(self.sems[it[1]], it[2])
                        else:
                            it[1](engh).then_inc(self.sems[it[2]], it[3])

                getattr(block, handles[e])(body)


class _Rec:
    def __getattr__(self, name):
        def mk(*a, **k):
            return lambda e: getattr(e, name)(*a, **k)
        return mk


C = _Rec()


def host_consts():
    s = np.arange(128)[:, None]
    c = np.arange(128)[None, :]
    same = (s // 32) == (c // 32)
    ind = ((np.arange(128)[:, None] // 32) == np.arange(4)[None, :]).astype(np.float32)
    out = {}
    for d, (le, lt) in enumerate([(lambda a, b: a <= b, lambda a, b: a > b), (lambda a, b: a >= b, lambda a, b: a < b)]):
        tri = (same & le(s, c)).astype(np.float32)
        trix = (same & lt(s, c)).astype(np.float32)
        out["tri%d" % d] = np.concatenate([tri, ind], axis=1)
        out["trix%d" % d] = trix
    out["ones"] = np.ones((128, 128), np.float32)
    out["negm"] = np.where(ind > 0, 0.0, -10000.0).astype(np.float32)
    return out


class K:
    def __init__(self, P, stop_after=None):
        self.P = P
        self.rr = 0
        self.stop_after = stop_after

    def ew(self):
        self.rr += 1
        return ("dve", "pool")[self.rr % 2]


def build_program(stop_after=None):
    nc = bass.Bass("TRN2", target_bir_lowering=False)
    st = ExitStack()
    P = Prog(nc, st)
    D = {}
    def din(name, shape, dt=F32):
        D[name] = P.dram(name, shape, dt, kind="ExternalInput")
        return D[name]
    xT = din("xT", [2, 8, 128, NT])
    cT = din("cT", [128, 8, 3])
    ada_w = din("ada_w", [2, 1024, 3072])
    ada_bT = din("ada_bT", [2, 128, 24])
    normgT = din("normgT", [2, 128, 8])
    fngT = din("fngT", [128, 8])
    ev_w_in = din("ev_w_in", [1024, 8224])
    ev_w_out = din("ev_w_out", [2048, 1024])
    gkw = din("gkw", [2, 16, 512])
    gkb = din("gkb", [2, 1, 512])
    glang = din("glang", [128, 2])
    hgng = din("hgng", [128, 1])
    lbraw = din("lbraw", [2, 2, 1024])
    lbrawT = din("lbrawT", [128, 2, 2, 8])
    consts = host_consts()
    for k, v in consts.items():
        din("c_" + k, list(v.shape))
    x1T = P.dram("x1T", [2, 8, 128, NT], F32, kind="ExternalOutput" if stop_after == 0 else "Internal")
    od_w_in = din("od_w_in", [1024, 4608])
    od_w_sw = din("od_w_sw", [1024, 1280])
    od_w_out = din("od_w_out", [2048, 1024])
    glu_w = din("glu_w", [1024, 2048])
    ropeC = din("ropeC", [64, 2048])
    ropeS = din("ropeS", [64, 2048])
    sinkT = din("sinkT", [128, 8])
    maskP_d = din("maskP", [128, 128])
    maskN_d = din("maskN", [128, 128])
    lamTre = din("lamTre", [128, 64])
    lamTim = din("lamTim", [128, 64])
    ldtT = din("ldtT", [128, 64])
    BTre = din("BTre", [128, 32, 16])
    BTim = din("BTim", [128, 32, 16])
    CTre = din("CTre", [128, 32, 16])
    CTim = din("CTim", [128, 32, 16])
    dskT = din("dskT", [128, 8])
    iota_d = din("iota", [128, 1152])
    outT = P.dram("outT", [2, 8, 128, 2048], F32, kind="ExternalOutput")
    mats_d = P.dram("mats_d", [64, 128, 512], BF16)
    zT_d = P.dram("zT_d", [8, 128, NT], BF16)
    aT_d = P.dram("aT_d", [16, 128, NT], BF16)

    ones = P.sbuf("ones", [128, 128], F32)
    tri = [P.sbuf("tri%d" % d, [128, 132], F32) for d in range(2)]
    trix = [P.sbuf("trix%d" % d, [128, 128], F32) for d in range(2)]
    for t, n in [(ones, "c_ones"), (tri[0], "c_tri0"), (tri[1], "c_tri1"), (trix[0], "c_trix0"), (trix[1], "c_trix1")]:
        P.dma("sp", t, t[:], D[n], D[n][:])
    hT = P.sbuf("hT", [128, 8, NT], BF16)
    ws = P.sbuf("ws", [128, 4608], F32)
    xblk = ws.view("xblk", (slice(None), slice(0, 2048)))
    sqb = ws.view("sqb", (slice(None), slice(2048, 4096)))
    rstd = P.sbuf("rstd", [128, 512], F32)
    tmpf = P.sbuf("tmpf", [128, 512], F32)
    tmpf2 = P.sbuf("tmpf2", [128, 512], F32)
    modL = [P.sbuf("mod%d" % i, [128, 24, 3], F32) for i in range(2)]
    mscL = [P.sbuf("msc%d" % i, [128, 8, 3], F32) for i in range(2)]
    mod, msc = modL[0], mscL[0]
    cTs = P.sbuf("cTs", [128, 8, 3], F32)
    silc = P.sbuf("silc", [128, 8, 3], F32)
    abT = P.sbuf("abT", [128, 24], F32)
    ngT = P.sbuf("ngT", [128, 8], F32)
    wada = [ws.view("wada0", (slice(None), slice(0, 3072)))] * 2
    qT = P.sbuf("qT", [128, NT], BF16)
    kT = [P.sbuf("kT%d" % d, [128, NT], BF16) for d in range(2)]
    ktok = [P.sbuf("ktok%d" % d, [128, NTILE, 128], BF16) for d in range(2)]
    vtok = P.sbuf("vtok", [128, NTILE, 256], BF16)
    gtok = [P.sbuf("gtok%d" % d, [128, NTILE, 128], F32) for d in range(2)]
    wA = P.sbuf("wA", [128, 8, 128], BF16)
    wB = P.sbuf("wB", [128, 8, 128], BF16)
    wC = P.sbuf("wC", [128, 8, 256], BF16)
    wD = P.sbuf("wD", [128, 8, 256], BF16)
    wE = P.sbuf("wE", [128, 8, 128], BF16)
    wout = T(P, "wout", hT[:].rearrange("p k n -> p (k n)")[:, 0:16384].rearrange("p (k n) -> p k n", k=16), "sbuf")
    wout.root = hT
    ablk = vtok.view("ablk", (slice(None), slice(0, 16), slice(None)))
    astage = P.sbuf("astage", [128, 512], BF16)
    lrT_all = P.sbuf("lrT", [48, NT], BF16)
    lrT = [lrT_all.view("lrT%d" % d, (slice(32 * d, 32 * d + 16), slice(None))) for d in range(2)]
    gkw_all = P.sbuf("gkw_s", [48, 512], BF16)
    gkw_s = [gkw_all.view("gkw_s%d" % d, (slice(32 * d, 32 * d + 16), slice(None))) for d in range(2)]
    gkb_s = [P.sbuf("gkb_s%d" % d, [1, 512], F32) for d in range(2)]
    lb_bc = P.sbuf("lb_bc", [128, 2, 1024], F32)
    oml_bc = P.sbuf("oml_bc", [128, 2, 1024], F32)
    lbtmp = T(P, "lbtmp", ws[:, 0:2048].rearrange("p (a b) -> p a b", a=2), "sbuf")
    lbT = P.sbuf("lbT", [128, 2, 2, 8], F32)
    omlT = P.sbuf("omlT", [128, 2, 8], F32)
    lbTt = P.sbuf("lbTt", [128, 2, 8], F32)
    glang_s = P.sbuf("glang_s", [128, 2], F32)
    hgng_s = P.sbuf("hgng_s", [128, 1], F32)
    EB = [P.sbuf("EB%d" % d, [128, 128], F32) for d in range(2)]
    EBn = [P.sbuf("EBn%d" % d, [128, 128], F32) for d in range(2)]
    decj2 = [[P.sbuf("decj%d_%d" % (d, i), [128, 4], F32) for i in range(2)] for d in range(2)]
    qd2 = [[P.sbuf("qd%d_%d" % (d, i), [128, 128], BF16) for i in range(2)] for d in range(2)]
    cur_par = [0]

    class _Par:
        def __init__(self, bufs):
            self.bufs = bufs

        def __getitem__(self, d):
            return self.bufs[d][cur_par[0]]
    decj = _Par(decj2)
    qd = _Par(qd2)
    ki = [P.sbuf("ki%d" % d, [128, 128], BF16) for d in range(2)]
    ktz = [P.sbuf("ktz%d" % d, [128, 4, 128], BF16) for d in range(2)]
    am = [P.sbuf("am%d" % d, [128, 128], BF16) for d in range(2)]
    S = [P.sbuf("S%d" % d, [128, 4, 256], F32) for d in range(2)]
    Sbf = [P.sbuf("Sbf%d" % d, [128, 4, 256], BF16) for d in range(2)]
    sgate = P.sbuf("sgate", [128, 512], F32)
    oi_sb = [P.sbuf("oi_sb%d" % d, [128, 256], F32) for d in range(2)]
    xres = P.sbuf("xres", [128, 512], F32)
    xo = P.sbuf("xo", [128, 512], F32)

    banks = [P.psum("bank%d" % i, [128, 512], F32) for i in range(8)]
    def pv(b, name, lo, hi):
        return banks[b].view(name, (slice(None), slice(lo, hi)))
    ps_b = [pv(d, "ps_b%d" % d, 0, 132) for d in range(2)]
    ps_d = [pv(d, "ps_d%d" % d, 256, 384) for d in range(2)]
    ps_a = [pv(2 + d, "ps_a%d" % d, 0, 128) for d in range(2)]
    ps_oi = [pv(4 + d, "ps_oi%d" % d, 0, 256) for d in range(2)]
    ps_oe = [pv(4 + d, "ps_oe%d" % d, 256, 512) for d in range(2)]
    ps_kv = [pv(6 + d, "ps_kv%d" % d, 0, 512) for d in range(2)]
    ps_p = [banks[6], banks[7], banks[0], banks[1], banks[2], banks[3], banks[4], banks[5]]
    pp = [0]

    pp_list = [ps_p]

    def next_pp():
        pp[0] = (pp[0] + 1) % len(pp_list[0])
        return pp_list[0][pp[0]]

    rr = [0]

    def ew():
        return "dve"

    def load_w(wt, src_t, col0, ncols, row0=0, nk=8, dcol=0):
        src = src_t[row0:row0 + nk * 128, col0:col0 + ncols].rearrange("(kt p) j -> p kt j", p=128)
        P.dma("pool", wt, wt[:, 0:nk, dcol:dcol + ncols], src_t, src)

    def proj_feat(wt, wcol0, M, blk, out_ps, nk=8, rhs_t=None, prow=0):
        c0, c1 = blk
        src = hT if rhs_t is None else rhs_t
        for kt in range(nk):
            P.op("pe", C.matmul(out_ps[prow:prow + M, 0:c1 - c0], lhsT=wt[:, kt, wcol0:wcol0 + M],
                                                 rhs=src[:, kt, c0:c1], start=(kt == 0), stop=(kt == nk - 1)),
                 [wt, src], [out_ps])

    def proj_tok(wt, wcol0, ncols, tile_i, out_ps):
        for kt in range(8):
            P.op("pe", C.matmul(out_ps[:, 0:ncols], lhsT=hT[:, kt, tile_i * 128:(tile_i + 1) * 128],
                                                 rhs=wt[:, kt, wcol0:wcol0 + ncols], start=(kt == 0), stop=(kt == 7)),
                 [wt, hT], [out_ps])

    def rstd_from_ps(ps, N, dim):
        P.op("act", C.activation(out=tmpf2[:, 0:N], in_=ps[:, 0:N], func=AF.Ln, scale=1.0 / dim, bias=epsb[:, 0:1]),
             [ps, epsb], [tmpf2])
        P.op("act", C.activation(out=rstd[:, 0:N], in_=tmpf2[:, 0:N], func=AF.Exp, scale=-0.5), [tmpf2], [rstd])

    epsb = P.sbuf("epsb", [128, 1], F32)
    P.op("pool", C.memset(epsb[:], EPS), [], [epsb])
    oneb = P.sbuf("oneb", [128, 1], F32)
    P.op("pool", C.memset(oneb[:], 1.0), [], [oneb])

    wada3 = []
    wi = [0]

    def adaln(L):
        P.dma("sp", cTs, cTs[:], cT, cT[:])
        P.dma("sp", abT, abT[:], ada_bT, ada_bT[L])
        P.dma("sp", ngT, ngT[:], normgT, normgT[L])
        P.op("act", C.activation(out=silc[:], in_=cTs[:], func=AF.Silu), [cTs], [silc])
        if not wada3:
            for i_ in range(3):
                wada3.append(T(P, "wada3_%d" % i_, ws[:, 1536 * i_:1536 * (i_ + 1)], "sbuf"))
        for kt in range(8):
            ps = next_pp()
            for half in range(2):
                wb = wada3[wi[0] % 3]; wi[0] += 1
                P.dma("sp", wb, wb[:], ada_w, ada_w[L, kt * 128:(kt + 1) * 128, half * 1536:(half + 1) * 1536])
                for j_ in range(12):
                    jt = half * 12 + j_
                    P.op("pe", C.matmul(ps[:, jt * 3:jt * 3 + 3], lhsT=wb[:, j_ * 128:(j_ + 1) * 128],
                                        rhs=silc[:, kt, :], start=True, stop=True), [wb, silc], [ps])
            if kt == 0:
                P.op("dve", C.tensor_tensor(out=mod[:], in0=ps[:, 0:72].rearrange("p (a b) -> p a b", b=3),
                                            in1=abT[:].unsqueeze(2).to_broadcast([128, 24, 3]), op=ALU.add),
                     [ps, abT], [mod])
            else:
                P.op("dve", C.tensor_tensor(out=mod[:], in0=ps[:, 0:72].rearrange("p (a b) -> p a b", b=3),
                                            in1=mod[:], op=ALU.add), [ps, mod], [mod])
        P.op("dve", C.scalar_tensor_tensor(out=msc[:], in0=mod[:, 8:16, :], scalar=1.0,
                                                     in1=ngT[:].unsqueeze(2).to_broadcast([128, 8, 3]),
                                                     op0=ALU.add, op1=ALU.mult), [mod, ngT], [msc])

    nm_t = []

    def norm_mod(xsrc, b):
        if not nm_t:
            nm_t.append(T(P, "n_x", ws[:, 0:4096].rearrange("p (k n) -> p k n", k=8), "sbuf"))
            nm_t.append(T(P, "n_sq0", gtok[0][:].rearrange("p a b -> p (a b)")[:, 0:2048].rearrange("p (k n) -> p k n", k=4), "sbuf"))
            nm_t.append(T(P, "n_sq1", gtok[1][:].rearrange("p a b -> p (a b)")[:, 0:2048].rearrange("p (k n) -> p k n", k=4), "sbuf"))
        xb_, sq0, sq1 = nm_t
        for (c0, c1) in BLOCKS:
            N = c1 - c0
            col = 2 if c0 == 0 else b
            P.dma("sp", xb_, xb_[:, :, 0:N], xsrc, xsrc[b, :, :, c0:c1].rearrange("k p n -> p k n"))
            P.op("act", C.activation(out=sq0[:, :, 0:N], in_=xb_[:, 0:4, 0:N], func=AF.Square), [xb_], [sq0])
            P.op("act", C.activation(out=sq1[:, :, 0:N], in_=xb_[:, 4:8, 0:N], func=AF.Square), [xb_], [sq1])
            ps = next_pp()
            for kt in range(8):
                sq = sq0 if kt < 4 else sq1
                P.op("pe", C.matmul(ps[:, 0:N], lhsT=ones[:], rhs=sq[:, kt % 4, 0:N], start=(kt == 0), stop=(kt == 7)), [ones, sq], [ps])
            rstd_from_ps(ps, N, 1024.0)
            for kt in range(8):
                P.op("dve", C.scalar_tensor_tensor(out=tmpf[:, 0:N], in0=xb_[:, kt, 0:N], scalar=msc[:, kt, col:col + 1],
                                                   in1=rstd[:, 0:N], op0=ALU.mult, op1=ALU.mult), [xb_, msc, rstd], [tmpf])
                P.op("act", C.activation(out=hT[:, kt, c0:c1], in_=tmpf[:, 0:N], func=AF.Identity,
                                         bias=mod[:, kt, col:col + 1], scale=1.0), [tmpf, mod], [hT])

    def scan_head(dv, kT_d, ktok_d, gtok_d):
        nm = dv // 128
        oacc = ws
        order = [list(range(NTILE)), [1, 0] + list(range(17, 1, -1))]
        P.op("pool", C.memset(oacc[:, 0:nm * NT], 0.0), [], [oacc])
        for d in range(2):
            P.op("pool", C.memset(S[d][:], 0.0), [], [S[d]])
            P.op("pool", C.memset(Sbf[d][:], 0.0), [], [Sbf[d]])
        nslot = 512 // dv

        def kv_mm(d, i, p):
            j = (p if d == 0 else 3 - p)
            sl = (p % nslot) * dv
            P.op("pe", C.matmul(ps_kv[d][:, sl:sl + dv], lhsT=ktz[d][:, j, :], rhs=vtok[:, i, 0:dv], start=True, stop=True),
                 [ktz[d], vtok], [ps_kv[d]])

        def s_upd(d, p):
            j = (p if d == 0 else 3 - p)
            sl = (p % nslot) * dv
            P.op("dve", C.scalar_tensor_tensor(out=S[d][:, (p + 1) % 4, 0:dv], in0=S[d][:, p, 0:dv],
                                               scalar=decj[d][:, j:j + 1], in1=ps_kv[d][:, sl:sl + dv],
                                               op0=ALU.mult, op1=ALU.add),
                 [S[d], decj[d], ps_kv[d]], [S[d]])

        def inter_mm(d, p):
            j = (p if d == 0 else 3 - p)
            for m in range(nm):
                P.op("pe", C.matmul(ps_oe[d][:, m * 128 + 32 * j:m * 128 + 32 * j + 32],
                                    lhsT=Sbf[d][:, p, m * 128:(m + 1) * 128],
                                    rhs=qd[d][:, 32 * j:32 * j + 32], start=True, stop=True),
                     [Sbf[d], qd[d]], [ps_oe[d]])

        def st1(d, i):
            g_ap = gtok_d[d][:, i, :]
            P.op("pe", C.matmul(ps_b[d][:], lhsT=g_ap, rhs=tri[d][:], start=True, stop=True), [gtok_d[d], tri[d]], [ps_b[d]])
            P.op("pe", C.matmul(ps_d[d][:], lhsT=trix[d][:], rhs=g_ap, start=True, stop=True), [gtok_d[d], trix[d]], [ps_d[d]])

        def st2(d, i):
            P.op("act", C.activation(out=EB[d][:], in_=ps_b[d][:, 0:128], func=AF.Exp), [ps_b[d]], [EB[d]])
            P.op("act", C.activation(out=EBn[d][:], in_=ps_b[d][:, 0:128], func=AF.Exp, scale=-1.0), [ps_b[d]], [EBn[d]])
            P.op("act", C.activation(out=decj[d][:], in_=ps_b[d][:, 128:132], func=AF.Exp), [ps_b[d]], [decj[d]])
            for j in range(4):
                P.op("act", C.activation(out=EDz[d][:, j, :], in_=ps_d[d][:], func=AF.Exp, bias=negm[:, j:j + 1], scale=1.0), [ps_d[d], negm], [EDz[d]])

        def st3(d, i):
            tsl = slice(i * 128, (i + 1) * 128)
            P.op("dve", C.tensor_tensor(out=qd[d][:], in0=qT[:, tsl], in1=EB[d][:], op=ALU.mult), [qT, EB[d]], [qd[d]])
            P.op("dve", C.tensor_tensor(out=ki[d][:], in0=kT_d[d][:, tsl], in1=EBn[d][:], op=ALU.mult), [kT_d[d], EBn[d]], [ki[d]])
            P.op("dve", C.tensor_tensor(out=ktz[d][:], in0=ktok_d[d][:, i, :].unsqueeze(1).to_broadcast([128, 4, 128]), in1=EDz[d][:], op=ALU.mult),
                 [ktok_d[d], EDz[d]], [ktz[d]])

        def st4(d, i):
            for p in range(nslot):
                kv_mm(d, i, p)
            P.op("pe", C.matmul(ps_a[d][:], lhsT=ki[d][:], rhs=qd[d][:], start=True, stop=True), [ki[d], qd[d]], [ps_a[d]])
            inter_mm(d, 0)

        def st5(d, i):
            for p in range(nslot):
                s_upd(d, p)
            P.op("dve", C.tensor_tensor(out=am[d][:], in0=ps_a[d][:], in1=tri[d][:, 0:128], op=ALU.mult), [ps_a[d], tri[d]], [am[d]])

        def st5b(d, i):
            if nslot < 4:
                for p in range(nslot, 4):
                    kv_mm(d, i, p)

        def st5c(d, i):
            if nslot < 4:
                for p in range(nslot, 4):
                    s_upd(d, p)

        def st6(d, i):
            P.op("act", C.copy(out=Sbf[d][:, :, 0:dv], in_=S[d][:, :, 0:dv]), [S[d]], [Sbf[d]])
            for m in range(nm):
                P.op("pe", C.matmul(ps_oi[d][:, m * 128:(m + 1) * 128], lhsT=vtok[:, i, m * 128:(m + 1) * 128],
                                    rhs=am[d][:], start=True, stop=True), [vtok, am[d]], [ps_oi[d]])

        def st7(d, i):
            P.op("act", C.copy(out=oi_sb[d][:, 0:dv], in_=ps_oi[d][:, 0:dv]), [ps_oi[d]], [oi_sb[d]])
            for p in range(1, 4):
                inter_mm(d, p)
            for m in range(nm):
                P.op("dve", C.tensor_tensor(out=oi_sb[d][:, m * 128:(m + 1) * 128], in0=oi_sb[d][:, m * 128:(m + 1) * 128],
                                            in1=oacc[:, m * NT + i * 128:m * NT + (i + 1) * 128], op=ALU.add),
                     [oi_sb[d], oacc], [oi_sb[d]])

        def st8(d, i):
            for m in range(nm):
                o_ap = oacc[:, m * NT + i * 128:m * NT + (i + 1) * 128]
                P.op("dve", C.tensor_tensor(out=o_ap, in0=ps_oe[d][:, m * 128:(m + 1) * 128],
                                            in1=oi_sb[d][:, m * 128:(m + 1) * 128], op=ALU.add),
                     [ps_oe[d], oi_sb[d]], [oacc])

        def run(stages, step):
            cur_par[0] = step % 2
            for stg in stages:
                for d in range(2):
                    stg(d, order[d][step])
        run((st1, st2, st3), 0)
        for step in range(NTILE):
            run((st4, st5, st5b, st5c), step)
            if step + 1 < NTILE:
                run((st1, st2, st3), step + 1)
            run((st6, st7, st8), step)

    def finalize_head(dv, wg, ng_t, kt_out0):
        nm = dv // 128
        oacc = ws
        for (c0, c1) in BLOCKS:
            N = c1 - c0
            for m in range(nm):
                P.op("dve", C.tensor_tensor(out=tmpf[:, 0:N] if m == 0 else tmpf2[:, 0:N], in0=oacc[:, m * NT + c0:m * NT + c1],
                                            in1=oacc[:, m * NT + c0:m * NT + c1], op=ALU.mult), [oacc], [tmpf if m == 0 else tmpf2])
            ps = next_pp()
            for m in range(nm):
                P.op("pe", C.matmul(ps[:, 0:N], lhsT=ones[:], rhs=(tmpf if m == 0 else tmpf2)[:, 0:N], start=(m == 0), stop=(m == nm - 1)),
                     [ones, tmpf if m == 0 else tmpf2], [ps])
            rstd_from_ps(ps, N, float(dv))
            for m in range(nm):
                P.op("dve", C.tensor_tensor(out=oacc[:, m * NT + c0:m * NT + c1], in0=oacc[:, m * NT + c0:m * NT + c1], in1=rstd[:, 0:N], op=ALU.mult),
                     [oacc, rstd], [oacc])
        for (c0, c1) in BLOCKS:
            N = c1 - c0
            for m in range(nm):
                psg = next_pp()
                proj_feat(wg, m * 128, 128, (c0, c1), psg)
                P.op("act", C.activation(out=sgate[:, 0:N], in_=psg[:, 0:N], func=AF.Silu), [psg], [sgate])
                P.op("dve", C.scalar_tensor_tensor(out=astage[:, 0:N], in0=oacc[:, m * NT + c0:m * NT + c1], scalar=ng_t[:, m:m + 1], in1=sgate[:, 0:N],
                                                   op0=ALU.mult, op1=ALU.mult), [oacc, ng_t, sgate], [astage])
                P.dma("sp", aT_d, aT_d[kt_out0 + m, :, c0:c1], astage, astage[:, 0:N])

    akt = []

    def out_proj_residual(wout_src, xsrc, xdst, b, nk):
        if not akt:
            srcs = [qT[:], kT[0][:], kT[1][:], ktok[0][:].rearrange("p a b -> p (a b)")]
            for si, ap_ in enumerate(srcs):
                for q4 in range(4):
                    akt.append(T(P, "akt%d" % (si * 4 + q4), ap_[:, 512 * q4:512 * (q4 + 1)], "sbuf"))
        load_w(wout, wout_src, 0, 1024, 0, nk)
        for (c0, c1) in BLOCKS:
            N = c1 - c0
            col = 2 if c0 == 0 else b
            for kt in range(nk):
                P.dma("sp", akt[kt], akt[kt][:, 0:N], aT_d, aT_d[kt, :, c0:c1])
            for jt in range(8):
                ps = next_pp()
                for kt in range(nk):
                    P.op("pe", C.matmul(ps[:, 0:N], lhsT=wout[:, kt, jt * 128:(jt + 1) * 128], rhs=akt[kt][:, 0:N],
                                        start=(kt == 0), stop=(kt == nk - 1)), [wout, akt[kt]], [ps])
                P.dma("sp", xres, xres[:, 0:N], xsrc, xsrc[b, jt, :, c0:c1])
                P.op("dve", C.scalar_tensor_tensor(out=xo[:, 0:N], in0=ps[:, 0:N], scalar=mod[:, 16 + jt, col:col + 1],
                                                   in1=xres[:, 0:N], op0=ALU.mult, op1=ALU.add), [ps, mod, xres], [xo])
                P.dma("sp", xdst, xdst[b, jt, :, c0:c1], xo, xo[:, 0:N])

    def layer0(b):
        norm_mod(xT, b)
        P.barrier()
        load_w(wA, ev_w_in, 2048, 32)
        for d in range(2):
            for blk in BLOCKS:
                ps = next_pp()
                proj_feat(wA, 16 * d, 16, blk, ps, prow=32 * d)
                P.op("act", C.copy(out=lrT[d][:, blk[0]:blk[1]], in_=ps[32 * d:32 * d + 16, 0:blk[1] - blk[0]]), [ps], [lrT[d]])
        hw = []
        for hh_ in range(4):
            hw.append(([(wA, hh_ * 128, 128), (wB, 512 + hh_ * 128, 128), (wC, 1024 + hh_ * 256, 256)], [(wD, 2080 + hh_ * 256, 256)]))
        for hh_ in range(8):
            hw.append(([(wA, 3104 + hh_ * 128, 128), (wC, 4128 + hh_ * 128, 128), (wC, 5152 + hh_ * 128, 128, 128), (wB, 6176 + hh_ * 128, 128)],
                       [(wD, 7200 + hh_ * 128, 128)]))

        def wload(idx, late):
            if idx < len(hw):
                for ent in hw[idx][1 if late else 0]:
                    load_w(ent[0], ev_w_in, ent[1], ent[2], dcol=(ent[3] if len(ent) > 3 else 0))
        wload(0, False)
        wload(0, True)
        for hh in range(4):
            for blk in BLOCKS:
                N = blk[1] - blk[0]
                ps = next_pp()
                proj_feat(wA, 0, 128, blk, ps)
                P.op("act", C.activation(out=qT[:, blk[0]:blk[1]], in_=ps[:, 0:N], func=AF.Copy, scale=128.0 ** -0.5), [ps], [qT])
                ps = next_pp()
                proj_feat(wB, 0, 128, blk, ps)
                P.op("dve", C.tensor_copy(out=kT[0][:, blk[0]:blk[1]], in_=ps[:, 0:N]), [ps], [kT[0]])
            for i in range(NTILE):
                ps = next_pp()
                proj_tok(wB, 0, 128, i, ps)
                P.op("act", C.copy(out=ktok[0][:, i, :], in_=ps[:, 0:128]), [ps], [ktok[0]])
                ps = next_pp()
                proj_tok(wC, 0, 256, i, ps)
                P.op("dve", C.tensor_copy(out=vtok[:, i, :], in_=ps[:, 0:256]), [ps], [vtok])
                for d in range(2):
                    ps = next_pp()
                    P.op("pe", C.matmul(ps[:, 0:128], lhsT=lrT[d][:, i * 128:(i + 1) * 128],
                                                                         rhs=gkw_s[d][:, hh * 128:(hh + 1) * 128], start=True, stop=False),
                         [lrT[d], gkw_s[d]], [ps])
                    P.op("pe", C.matmul(ps[:, 0:128], lhsT=ones[0:1, :], rhs=gkb_s[d][:, hh * 128:(hh + 1) * 128],
                                                                    start=False, stop=True), [ones, gkb_s[d]], [ps])
                    P.op("act", C.activation(out=tmpf[:, 0:128], in_=ps[:, 0:128], func=AF.Exp, scale=-1.0), [ps], [tmpf])
                    P.op("act", C.activation(out=tmpf2[:, 0:128], in_=tmpf[:, 0:128], func=AF.Ln, bias=oneb[:, 0:1], scale=1.0), [tmpf, oneb], [tmpf2])
                    P.op("dve", C.tensor_scalar(out=gtok[d][:, i, :], in0=tmpf2[:, 0:128], scalar1=-1.0 / 16.0, scalar2=None, op0=ALU.mult),
                         [tmpf2], [gtok[d]])
            wload(hh + 1, False)
            scan_head(256, [kT[0], kT[0]], [ktok[0], ktok[0]], gtok)
            finalize_head(256, wD, glang_s, hh * 2)
            wload(hh + 1, True)
            if stop_after == "l0c":
                return
        for hh in range(8):
            wf = [wB, wE]
            for blk in BLOCKS:
                N = blk[1] - blk[0]
                ps = next_pp()
                proj_feat(wA, 0, 128, blk, ps)
                P.op("act", C.copy(out=qT[:, blk[0]:blk[1]], in_=ps[:, 0:N]), [ps], [qT])
                for d in range(2):
                    ps = next_pp()
                    proj_feat(wC, 128 * d, 128, blk, ps)
                    P.op("act", C.activation(out=tmpf[:, 0:N], in_=ps[:, 0:N], func=AF.Sigmoid, scale=-1.0), [ps], [tmpf])
                    P.op("dve", C.tensor_scalar(out=kT[d][:, blk[0]:blk[1]], in0=tmpf[:, 0:N], scalar1=omlT[:, d, hh:hh + 1],
                                                                                 scalar2=None, op0=ALU.mult), [tmpf, omlT], [kT[d]])
            for i in range(NTILE):
                ps = next_pp()
                proj_tok(wB, 0, 128, i, ps)
                P.op("act", C.copy(out=vtok[:, i, 0:128], in_=ps[:, 0:128]), [ps], [vtok])
                psf = next_pp()
                proj_tok(wC, 0, 256, i, psf)
                P.op("act", C.activation(out=tmpf[:, 0:256], in_=psf[:, 0:256], func=AF.Sigmoid), [psf], [tmpf])
                P.op("dve", C.tensor_tensor(out=tmpf2[:, 0:256].rearrange("p (d c) -> p d c", d=2), in0=tmpf[:, 0:256].rearrange("p (d c) -> p d c", d=2),
                                            in1=oml_bc[:, :, hh * 128:(hh + 1) * 128], op=ALU.mult), [tmpf, oml_bc], [tmpf2])
                for d in range(2):
                    P.op("pool", C.tensor_tensor(out=ktok[d][:, i, :], in0=oml_bc[:, d, hh * 128:(hh + 1) * 128], in1=tmpf2[:, d * 128:(d + 1) * 128], op=ALU.subtract),
                         [tmpf2, oml_bc], [ktok[d]])
                    P.op("dve", C.tensor_tensor(out=gtok[d][:, i, :], in0=tmpf2[:, d * 128:(d + 1) * 128], in1=lb_bc[:, d, hh * 128:(hh + 1) * 128], op=ALU.add),
                         [tmpf2, lb_bc], [gtok[d]])
            for d in range(2):
                P.op("act", C.activation(out=gtok[d][:], in_=gtok[d][:], func=AF.Ln), [gtok[d]], [gtok[d]])
            wload(4 + hh + 1, False)
            scan_head(128, kT, ktok, gtok)
            finalize_head(128, wD, hgng_s, 8 + hh)
            wload(4 + hh + 1, True)
        P.barrier()
        out_proj_residual(ev_w_out, xT, x1T, b, 16)
        P.barrier()


    def carve(name, ap):
        return T(P, name, ap, "sbuf")
    I32 = mybir.dt.int32
    gt0f = gtok[0][:].rearrange("p a b -> p (a b)")
    gt1f = gtok[1][:].rearrange("p a b -> p (a b)")
    kt0f = ktok[0][:].rearrange("p a b -> p (a b)")
    kt1f = ktok[1][:].rearrange("p a b -> p (a b)")
    CH = 1152
    s_Dre = carve("s_Dre", ws[:, 0:CH]); s_Dim = carve("s_Dim", ws[:, CH:2 * CH])
    s_t1 = carve("s_t1", ws[:, 2 * CH:3 * CH]); s_t2 = carve("s_t2", ws[:, 3 * CH:4 * CH])
    s_c = carve("s_c", gt0f[:, 0:CH]); s_s = carve("s_s", gt0f[:, CH:2 * CH])
    s_y = carve("s_y", gt1f)
    s_ub = carve("s_ub", qT[:])
    s_Xre = carve("s_Xre", kt0f[:, 0:CH]); s_Xim = carve("s_Xim", kt0f[:, CH:2 * CH])
    s_ki = P.sbuf("s_ki", [128, CH], I32)
    s_tt = P.sbuf("s_tt", [128, CH], F32)
    s_mats = [P.sbuf("s_mats%d" % i, [128, 512], BF16) for i in range(2)]
    thn = P.sbuf("thn", [128, 64], F32)
    rho = P.sbuf("rho", [128, 64], F32)
    offB = P.sbuf("offB", [128, 64], F32)
    dsk = P.sbuf("dsk", [128, 8], F32)
    esk = P.sbuf("esk", [128, 8], F32)
    maskP = P.sbuf("maskP_s", [128, 128], BF16)
    maskN = P.sbuf("maskN_s", [128, 128], BF16)
    onesK = P.sbuf("onesK", [128, 64], BF16)
    fng = P.sbuf("fng", [128, 8], F32)
    zero_c = P.sbuf("zero_c", [128, 1], F32)
    hpi = P.sbuf("hpi", [128, 1], F32)
    q25 = P.sbuf("q25", [128, 1], F32)
    a_kT = carve("a_kT", kT[0][0:64, :])
    a_qr = carve("a_qr", kT[1][0:64, 0:2048])
    a_vt = carve("a_vt", ktok[0][:, :, 0:64])
    a_sg = carve("a_sg", vtok[:, 0:16, :])
    a_cos = carve("a_cos", gt0f[0:64, 0:2048])
    a_sin = carve("a_sin", gt1f[0:64, 0:2048])
    PT = [carve("PT%d" % i, kt1f[:, 512 * i:512 * (i + 1)]) for i in range(4)] + \
         [carve("PT%d" % (4 + i), qT[:, 512 * i:512 * (i + 1)]) for i in range(4)]
    zblk = carve("zblk", vtok[:].rearrange("p a b -> p (a b)")[:, 0:4096].rearrange("p (k n) -> p k n", k=8))
    LATB = [(256, 768), (768, 1280), (1280, 1792), (1792, 2304)]
    S5B = [(0, 256), (256, 768), (768, 1152), (1152, 1408), (1408, 1920), (1920, 2304)]

    def s5_prep():
        lre = carve("p_lre", ws[:, 0:64]); lim = carve("p_lim", ws[:, 64:128]); ldt = carve("p_ldt", ws[:, 128:192])
        ar = carve("p_ar", ws[:, 192:256]); ai = carve("p_ai", ws[:, 256:320]); den = carve("p_den", ws[:, 320:384])
        cr = carve("p_cr", ws[:, 384:448]); ci = carve("p_ci", ws[:, 448:512]); tA = carve("p_tA", ws[:, 512:576]); tB = carve("p_tB", ws[:, 576:640])
        lamre = carve("p_lamre", ws[:, 640:704]); lamim = carve("p_lamim", ws[:, 704:768])
        kiI = s_ki
        Bre = carve("p_Bre", ws[:, 1024:1536].rearrange("p (a b) -> p a b", b=16)); Bim = carve("p_Bim", ws[:, 1536:2048].rearrange("p (a b) -> p a b", b=16))
        Cre = carve("p_Cre", ws[:, 2048:2560].rearrange("p (a b) -> p a b", b=16)); Cim = carve("p_Cim", ws[:, 2560:3072].rearrange("p (a b) -> p a b", b=16))
        Bbr = carve("p_Bbr", ws[:, 3072:3584].rearrange("p (a b) -> p a b", b=16)); Bbi = carve("p_Bbi", ws[:, 3584:4096].rearrange("p (a b) -> p a b", b=16))
        tC = carve("p_tC", ws[:, 4096:4608].rearrange("p (a b) -> p a b", b=16))
        inT = carve("p_inT", gt0f[:, 0:128]); stg = carve("p_stg", gt0f[:, 128:640]); stgb = s_mats[0]
        idn = carve("p_idn", gt0f[:, 640:768])
        P.dma("sp", lamre, lamre[:], lamTre, lamTre[:]); P.dma("sp", lamim, lamim[:], lamTim, lamTim[:]); P.dma("sp", ldt, ldt[:], ldtT, ldtT[:])
        for t_, d_ in ((Bre, BTre), (Bim, BTim), (Cre, CTre), (Cim, CTim)):
            P.dma("sp", t_, t_[:], d_, d_[:])
        P.dma("sp", dsk, dsk[:], dskT, dskT[:]); P.dma("sp", s_tt, s_tt[:], iota_d, iota_d[:])
        P.op("pool", C.memset(zero_c[:], 0.0), [], [zero_c]); P.op("pool", C.memset(hpi[:], float(np.pi / 2)), [], [hpi]); P.op("pool", C.memset(q25[:], 0.25), [], [q25])
        P.op("pool", C.memset(idn[:], 0.0), [], [idn])
        P.op("pool", C.affine_select(out=idn[:], in_=idn[:], pattern=[[-1, 128]], compare_op=ALU.not_equal, fill=1.0, base=0, channel_multiplier=1), [idn], [idn])
        P.op("act", C.activation(out=ldt[:], in_=ldt[:], func=AF.Exp), [ldt], [ldt])
        P.op("dve", C.tensor_tensor(out=lre[:], in0=lamre[:], in1=ldt[:], op=ALU.mult), [lamre, ldt], [lre])
        P.op("dve", C.tensor_tensor(out=lim[:], in0=lamim[:], in1=ldt[:], op=ALU.mult), [lamim, ldt], [lim])
        P.op("act", C.activation(out=rho[:], in_=lre[:], func=AF.Exp), [lre], [rho])
        P.op("dve", C.tensor_scalar(out=thn[:], in0=lim[:], scalar1=float(1.0 / (2 * np.pi)), scalar2=None, op0=ALU.mult), [lim], [thn])
        P.op("dve", C.tensor_scalar(out=offB[:], in0=thn[:], scalar1=float(CH), scalar2=None, op0=ALU.mult), [thn], [offB])

        def sincos(outs_, outc_, turns, tmp):
            P.op("dve", C.tensor_copy(out=kiI[:, 0:64], in_=turns[:]), [turns], [kiI])
            P.op("dve", C.tensor_copy(out=tmp[:], in_=kiI[:, 0:64]), [kiI], [tmp])
            P.op("dve", C.tensor_tensor(out=tmp[:], in0=turns[:], in1=tmp[:], op=ALU.subtract), [turns, tmp], [tmp])
            P.op("act", C.activation(out=outs_[:], in_=tmp[:], func=AF.Sin, scale=float(2 * np.pi), bias=zero_c[:, 0:1]), [tmp, zero_c], [outs_])
            P.op("dve", C.tensor_scalar(out=turns[:], in0=turns[:], scalar1=0.25, scalar2=None, op0=ALU.add), [turns], [turns])
            P.op("dve", C.tensor_copy(out=kiI[:, 0:64], in_=turns[:]), [turns], [kiI])
            P.op("dve", C.tensor_copy(out=tmp[:], in_=kiI[:, 0:64]), [kiI], [tmp])
            P.op("dve", C.tensor_tensor(out=tmp[:], in0=turns[:], in1=tmp[:], op=ALU.subtract), [turns, tmp], [tmp])
            P.op("act", C.activation(out=outc_[:], in_=tmp[:], func=AF.Sin, scale=float(2 * np.pi), bias=zero_c[:, 0:1]), [tmp, zero_c], [outc_])
        P.op("dve", C.tensor_copy(out=tA[:], in_=thn[:]), [thn], [tA])
        sincos(ai, ar, tA, tB)
        P.op("dve", C.tensor_tensor(out=ar[:], in0=ar[:], in1=rho[:], op=ALU.mult), [ar, rho], [ar])
        P.op("dve", C.tensor_tensor(out=ai[:], in0=ai[:], in1=rho[:], op=ALU.mult), [ai, rho], [ai])
        P.op("dve", C.tensor_scalar(out=ar[:], in0=ar[:], scalar1=-1.0, scalar2=None, op0=ALU.add), [ar], [ar])
        P.op("dve", C.tensor_tensor(out=den[:], in0=lamre[:], in1=lamre[:], op=ALU.mult), [lamre], [den])
        P.op("dve", C.tensor_tensor(out=tA[:], in0=lamim[:], in1=lamim[:], op=ALU.mult), [lamim], [tA])
        P.op("dve", C.tensor_tensor(out=den[:], in0=den[:], in1=tA[:], op=ALU.add), [den, tA], [den])
        P.op("dve", C.reciprocal(out=den[:], in_=den[:]), [den], [den])
        P.op("dve", C.tensor_tensor(out=cr[:], in0=ar[:], in1=lamre[:], op=ALU.mult), [ar, lamre], [cr])
        P.op("dve", C.tensor_tensor(out=tA[:], in0=ai[:], in1=lamim[:], op=ALU.mult), [ai, lamim], [tA])
        P.op("dve", C.tensor_tensor(out=cr[:], in0=cr[:], in1=tA[:], op=ALU.add), [cr, tA], [cr])
        P.op("dve", C.tensor_tensor(out=cr[:], in0=cr[:], in1=den[:], op=ALU.mult), [cr, den], [cr])
        P.op("dve", C.tensor_tensor(out=ci[:], in0=ai[:], in1=lamre[:], op=ALU.mult), [ai, lamre], [ci])
        P.op("dve", C.tensor_tensor(out=tA[:], in0=ar[:], in1=lamim[:], op=ALU.mult), [ar, lamim], [tA])
        P.op("dve", C.tensor_tensor(out=ci[:], in0=ci[:], in1=tA[:], op=ALU.subtract), [ci, tA], [ci])
        P.op("dve", C.tensor_tensor(out=ci[:], in0=ci[:], in1=den[:], op=ALU.mult), [ci, den], [ci])
        for dr in range(2):
            crb = cr[:, dr * 32:(dr + 1) * 32].unsqueeze(2).to_broadcast([128, 32, 16])
            cib = ci[:, dr * 32:(dr + 1) * 32].unsqueeze(2).to_broadcast([128, 32, 16])
            P.op("dve", C.tensor_tensor(out=Bbr[:], in0=Bre[:], in1=crb, op=ALU.mult), [Bre, cr], [Bbr])
            P.op("dve", C.tensor_tensor(out=tC[:], in0=Bim[:], in1=cib, op=ALU.mult), [Bim, ci], [tC])
            P.op("dve", C.tensor_tensor(out=Bbr[:], in0=Bbr[:], in1=tC[:], op=ALU.subtract), [Bbr, tC], [Bbr])
            P.op("dve", C.tensor_tensor(out=Bbi[:], in0=Bim[:], in1=crb, op=ALU.mult), [Bim, cr], [Bbi])
            P.op("dve", C.tensor_tensor(out=tC[:], in0=Bre[:], in1=cib, op=ALU.mult), [Bre, ci], [tC])
            P.op("dve", C.tensor_tensor(out=Bbi[:], in0=Bbi[:], in1=tC[:], op=ALU.add), [Bbi, tC], [Bbi])
            for pr in range(32):
                c0 = 32 * (pr % 4)
                for wi, src in enumerate((Bbr, Bbi)):
                    P.op("pool", C.memset(inT[:], 0.0), [], [inT])
                    P.op("dve", C.tensor_copy(out=inT[0:64, c0:c0 + 16], in_=src[0:64, pr, :]), [src], [inT])
                    P.op("dve", C.tensor_copy(out=inT[64:128, c0 + 16:c0 + 32], in_=src[64:128, pr, :]), [src], [inT])
                    ps = next_pp()
                    P.op("pe", C.transpose(ps[:, 0:128], inT[:], idn[:]), [inT, idn], [ps])
                    P.op("act", C.copy(out=stgb[:, wi * 128:(wi + 1) * 128], in_=ps[:, 0:128]), [ps], [stgb])
                P.op("pool", C.memset(stg[:, 0:256], 0.0), [], [stg])
                P.op("dve", C.tensor_copy(out=stg[0:64, c0:c0 + 16], in_=Cre[0:64, pr, :]), [Cre], [stg])
                P.op("dve", C.tensor_copy(out=stg[64:128, c0 + 16:c0 + 32], in_=Cre[64:128, pr, :]), [Cre], [stg])
                P.op("dve", C.tensor_scalar(out=stg[0:64, 128 + c0:128 + c0 + 16], in0=Cim[0:64, pr, :], scalar1=-1.0, scalar2=None, op0=ALU.mult), [Cim], [stg])
                P.op("dve", C.tensor_scalar(out=stg[64:128, 128 + c0 + 16:128 + c0 + 32], in0=Cim[64:128, pr, :], scalar1=-1.0, scalar2=None, op0=ALU.mult), [Cim], [stg])
                P.op("act", C.copy(out=stgb[:, 256:512], in_=stg[:, 0:256]), [stg], [stgb])
                P.dma("sp", mats_d, mats_d[dr * 32 + pr], stgb, stgb[:])

    s_tabs = [(s_c, s_s), (P.sbuf("s_c1", [128, CH], F32), P.sbuf("s_s1", [128, CH], F32))]
    s_tA = P.sbuf("s_tA", [128, CH], F32)
    EDz = [carve("EDz%d" % d, s_tA[:, 512 * d:512 * (d + 1)].rearrange("p (j c) -> p j c", j=4)) for d in range(2)]
    negm = P.sbuf("negm", [128, 4], F32)
    P.dma("sp", negm, negm[:], D["c_negm"], D["c_negm"][:])

    def s5_tab_a1(col, chunk, tb):
        if chunk == 0:
            P.op("dve", C.tensor_scalar(out=s_tA[:], in0=s_tt[:], scalar1=thn[:, col:col + 1], scalar2=None, op0=ALU.mult), [s_tt, thn], [s_tA])
        else:
            P.op("dve", C.tensor_scalar(out=s_tA[:], in0=s_tt[:], scalar1=thn[:, col:col + 1], scalar2=offB[:, col:col + 1], op0=ALU.mult, op1=ALU.add),
                 [s_tt, thn, offB], [s_tA])
        P.op("dve", C.tensor_copy(out=s_ki[:], in_=s_tA[:]), [s_tA], [s_ki])

    def s5_tab_a2(tb):
        cc, ss = s_tabs[tb]
        P.op("act", C.copy(out=ss[:], in_=s_ki[:]), [s_ki], [ss])

    def s5_tab_a(col, chunk, tb):
        s5_tab_a1(col, chunk, tb)
        s5_tab_a2(tb)

    def s5_tab_b(tb):
        cc, ss = s_tabs[tb]
        P.op("dve", C.tensor_tensor(out=s_tA[:], in0=s_tA[:], in1=ss[:], op=ALU.subtract), [s_tA, ss], [s_tA])
        P.op("act", C.activation(out=ss[:], in_=s_tA[:], func=AF.Sin, scale=float(2 * np.pi), bias=zero_c[:, 0:1]), [s_tA, zero_c], [ss])
        P.op("act", C.activation(out=s_tA[:], in_=s_tA[:], func=AF.Abs), [s_tA], [s_tA])
        P.op("act", C.activation(out=cc[:], in_=s_tA[:], func=AF.Sin, scale=float(-2 * np.pi), bias=hpi[:, 0:1]), [s_tA, hpi], [cc])

    def s5_layer(b):
        uw = [wA, wB]
        load_w(uw[0], od_w_in, 2560, 128)
        for o in range(8):
            if o + 1 < 8:
                load_w(uw[(o + 1) % 2], od_w_in, 2560 + (o + 1) * 128, 128)
            for blk in BLOCKS:
                N = blk[1] - blk[0]
                ps = next_pp()
                proj_feat(uw[o % 2], 0, 128, blk, ps)
                P.op("act", C.copy(out=s_ub[:, blk[0]:blk[1]], in_=ps[:, 0:N]), [ps], [s_ub])
                P.op("dve", C.tensor_scalar(out=s_y[:, blk[0]:blk[1]], in0=ps[:, 0:N], scalar1=dsk[:, o:o + 1], scalar2=None, op0=ALU.mult), [ps, dsk], [s_y])
            units = [(dr, pq, ch) for dr in range(2) for pq in range(4) for ch in range(2)]
            ybank = {S5B[0]: (banks[0], 0), S5B[3]: (banks[0], 256), S5B[1]: (banks[1], 0), S5B[2]: (banks[2], 0),
                     S5B[4]: (banks[3], 0), S5B[5]: (banks[4], 0)}
            for bi_ in range(5):
                P.op("dve", C.memset(banks[bi_][:], 0.0), [], [banks[bi_]])
            pp_list[0] = [banks[5], banks[6], banks[7]]
            def uctx(ui):
                dr, pq, ch = units[ui]
                col = dr * 32 + o * 4 + pq
                mtt = s_mats[(ui // 2) % 2]
                if dr == 0:
                    chunks = [[S5B[0], S5B[1], S5B[2]], [S5B[3], S5B[4], S5B[5]]]
                else:
                    chunks = [[S5B[0], S5B[4], S5B[5]], [S5B[1], S5B[2], S5B[3]]]

                def tau_slice(blk):
                    c0, c1 = blk
                    if dr == 0:
                        lo = c0 - ch * CH
                        return slice(lo, lo + (c1 - c0))
                    t_hi = (255 - c0) if c1 <= 256 else (2559 - c0)
                    t_lo = (255 - (c1 - 1)) if c1 <= 256 else (2559 - (c1 - 1))
                    hi = t_hi - ch * CH; lo = t_lo - ch * CH
                    return slice(hi, lo - 1 if lo > 0 else None, -1)
                return dr, pq, ch, col, mtt, chunks[ch], tau_slice

            dps = {}

            def Dmm(ui, half):
                dr, pq, ch, col, mtt, blks, tsl = uctx(ui)
                if ch == 0 and half == 0:
                    P.dma("sp", mtt, mtt[:], mats_d, mats_d[col])
                lst = []
                for blk in blks:
                    N = blk[1] - blk[0]
                    ps = next_pp()
                    P.op("pe", C.matmul(ps[:, 0:N], lhsT=mtt[:, half * 128:(half + 1) * 128], rhs=s_ub[:, blk[0]:blk[1]], start=True, stop=True),
                         [mtt, s_ub], [ps])
                    lst.append((blk, ps))
                dps[(ui, half)] = lst

            def Devac(ui, half):
                dr, pq, ch, col, mtt, blks, tsl = uctx(ui)
                dst = s_Dre if half == 0 else s_Dim
                for (blk, ps) in dps.pop((ui, half)):
                    N = blk[1] - blk[0]
                    P.op("act", C.copy(out=dst[:, tsl(blk)], in_=ps[:, 0:N]), [ps], [dst])

            def readout(ui):
                dr, pq, ch, col, mt, blks, tau_slice = uctx(ui)
                for blk in blks:
                    if blk[0] == 0:
                        continue
                    N = blk[1] - blk[0]
                    sl = tau_slice(blk)
                    if dr == 1:
                        P.op("act", C.copy(out=xst[:, 0:N], in_=s_Xre[:, sl]), [s_Xre], [xst])
                        P.op("act", C.copy(out=xst[:, 512:512 + N], in_=s_Xim[:, sl]), [s_Xim], [xst])
                        r_re, r_im, rt = xst[:, 0:N], xst[:, 512:512 + N], [xst]
                    else:
                        r_re, r_im, rt = s_Xre[:, sl], s_Xim[:, sl], [s_Xre, s_Xim]
                    yb, yo = ybank[blk]
                    P.op("pe", C.matmul(yb[:, yo:yo + N], lhsT=mt[:, 256:384], rhs=r_re, start=False, stop=False, skip_group_check=True), [mt] + rt, [yb])
                    P.op("pe", C.matmul(yb[:, yo:yo + N], lhsT=mt[:, 384:512], rhs=r_im, start=False, stop=False, skip_group_check=True), [mt] + rt, [yb])

            pend_ro = []
            s5_tab_a(units[0][0] * 32 + o * 4 + units[0][1], units[0][2], 0)
            s5_tab_b(0)
            Dmm(0, 0)
            for ui in range(len(units)):
                dr, pq, ch, col, mt, blks, tau_slice = uctx(ui)
                cc, ss = s_tabs[ui % 2]
                nxt = units[ui + 1] if ui + 1 < len(units) else None
                rho_b = rho[:, col:col + 1].to_broadcast([128, CH])
                Devac(ui, 0)
                Dmm(ui, 1)
                Devac(ui, 1)
                if nxt is not None:
                    ncol = nxt[0] * 32 + o * 4 + nxt[1]
                    P.op("act", C.activation(out=s_tA[:], in_=s_tt[:], func=AF.Identity, scale=thn[:, ncol:ncol + 1],
                                             bias=(zero_c[:, 0:1] if nxt[2] == 0 else offB[:, ncol:ncol + 1])), [s_tt, thn, offB, zero_c], [s_tA])
                    P.op("act", C.copy(out=s_ki[:], in_=s_tA[:]), [s_tA], [s_ki])
                    s5_tab_a2((ui + 1) % 2)
                if pend_ro:
                    readout(pend_ro.pop(0))
                P.op("dve", C.tensor_tensor(out=s_t2[:], in0=ss[:], in1=s_Dre[:], op=ALU.mult), [ss, s_Dre], [s_t2])
                P.op("dve", C.tensor_tensor(out=s_Dre[:], in0=cc[:], in1=s_Dre[:], op=ALU.mult), [cc, s_Dre], [s_Dre])
                P.op("dve", C.tensor_tensor(out=s_t1[:], in0=ss[:], in1=s_Dim[:], op=ALU.mult), [ss, s_Dim], [s_t1])
                P.op("dve", C.tensor_tensor(out=s_Dim[:], in0=cc[:], in1=s_Dim[:], op=ALU.mult), [cc, s_Dim], [s_Dim])
                P.op("dve", C.tensor_tensor(out=s_Dre[:], in0=s_Dre[:], in1=s_t1[:], op=ALU.add), [s_Dre, s_t1], [s_Dre])
                P.op("dve", C.tensor_tensor(out=s_Dim[:], in0=s_Dim[:], in1=s_t2[:], op=ALU.subtract), [s_Dim, s_t2], [s_Dim])
                if nxt is not None:
                    s5_tab_b((ui + 1) % 2)
                for (src, dst, zi) in ((s_Dre, s_t1, 0), (s_Dim, s_t2, 1)):
                    init = 0.0 if ch == 0 else zst[:, zi:zi + 1]
                    rd = [src, rho] + ([] if ch == 0 else [zst])
                    P.op("dve", C.tensor_tensor_scan(out=dst[:], data0=rho_b, data1=src[:], initial=init, op0=ALU.mult, op1=ALU.add), rd, [dst])
                if ch == 0:
                    P.op("dve", C.tensor_copy(out=zst[:, 0:1], in_=s_t1[:, CH - 1:CH]), [s_t1], [zst])
                    P.op("dve", C.tensor_copy(out=zst[:, 1:2], in_=s_t2[:, CH - 1:CH]), [s_t2], [zst])
                if nxt is not None:
                    Dmm(ui + 1, 0)
                bl = slice(256, CH) if ch == 0 else slice(0, CH)
                P.op("dve", C.tensor_tensor(out=s_Dre[:, bl], in0=ss[:, bl], in1=s_t2[:, bl], op=ALU.mult), [ss, s_t2], [s_Dre])
                P.op("dve", C.tensor_tensor(out=s_Dim[:, bl], in0=ss[:, bl], in1=s_t1[:, bl], op=ALU.mult), [ss, s_t1], [s_Dim])
                P.op("dve", C.tensor_tensor(out=s_t1[:, bl], in0=cc[:, bl], in1=s_t1[:, bl], op=ALU.mult), [cc, s_t1], [s_t1])
                P.op("dve", C.tensor_tensor(out=s_Xre[:, bl], in0=s_t1[:, bl], in1=s_Dre[:, bl], op=ALU.subtract), [s_t1, s_Dre], [s_Xre])
                P.op("dve", C.tensor_tensor(out=s_t2[:, bl], in0=cc[:, bl], in1=s_t2[:, bl], op=ALU.mult), [cc, s_t2], [s_t2])
                P.op("dve", C.tensor_tensor(out=s_Xim[:, bl], in0=s_t2[:, bl], in1=s_Dim[:, bl], op=ALU.add), [s_t2, s_Dim], [s_Xim])
                pend_ro.append(ui)
            readout(pend_ro.pop(0))
            pp_list[0] = ps_p
            for blk in S5B:
                N = blk[1] - blk[0]
                yb, yo = ybank[blk]
                P.op("dve", C.tensor_tensor(out=s_y[:, blk[0]:blk[1]], in0=yb[:, yo:yo + N], in1=s_y[:, blk[0]:blk[1]], op=ALU.add), [yb, s_y], [s_y])
                P.op("act", C.activation(out=astage[:, 0:N], in_=s_y[:, blk[0]:blk[1]], func=AF.Gelu), [s_y], [astage])
                P.dma("sp", zT_d, zT_d[o, :, blk[0]:blk[1]], astage, astage[:, 0:N])

    zst = P.sbuf("zst", [128, 2], F32)
    xst = carve("xst", Sbf[0][:].rearrange("p a b -> p (a b)"))

    def attn_layer(b):
        P.dma("sp", a_cos, a_cos[:], ropeC, ropeC[:])
        P.dma("sp", a_sin, a_sin[:], ropeS, ropeS[:])
        pti = [0]
        sbank = [0]
        for kvh in range(4):
            load_w(wA, od_w_in, 1024 + kvh * 64, 64, dcol=0)
            load_w(wA, od_w_sw, 1024 + kvh * 64, 64, dcol=64)
            load_w(wB, od_w_in, 1280 + kvh * 64, 64)
            load_w(wC, od_w_in, kvh * 256, 256)
            load_w(wD, od_w_sw, kvh * 256, 256)
            for blk in BLOCKS:
                N = blk[1] - blk[0]
                ps = next_pp()
                proj_feat(wA, 0, 64, blk, ps)
                if blk[0] == 0:
                    P.op("act", C.copy(out=a_kT[:, 0:256], in_=ps[0:64, 0:N]), [ps], [a_kT])
                    continue
                lc = slice(blk[0] - 256, blk[1] - 256)
                P.op("dve", C.tensor_tensor(out=tmpf[0:64, 0:N], in0=ps[0:64, 0:N], in1=a_cos[:, lc], op=ALU.mult), [ps, a_cos], [tmpf])
                ps2 = next_pp()
                proj_feat(wA, 64, 64, blk, ps2)
                P.op("dve", C.tensor_tensor(out=tmpf2[0:64, 0:N], in0=ps2[0:64, 0:N], in1=a_sin[:, lc], op=ALU.mult), [ps2, a_sin], [tmpf2])
                P.op("pool", C.tensor_tensor(out=a_kT[:, blk[0]:blk[1]], in0=tmpf[0:64, 0:N], in1=tmpf2[0:64, 0:N], op=ALU.add), [tmpf, tmpf2], [a_kT])
            for i in range(NTILE):
                ps = next_pp()
                proj_tok(wB, 0, 64, i, ps)
                P.op("act", C.copy(out=a_vt[:, i, :], in_=ps[:, 0:64]), [ps], [a_vt])
            for gh in range(2):
                load_w(wE, od_w_in, 1536 + kvh * 256 + gh * 128, 128)
                for bi, blk in enumerate(LATB):
                    ps = next_pp()
                    proj_feat(wE, 0, 128, blk, ps)
                    P.op("act", C.activation(out=a_sg[:, 4 * bi:4 * bi + 4, gh * 128:(gh + 1) * 128],
                                             in_=ps[:, 0:512].rearrange("p (a b) -> p a b", b=128), func=AF.Silu), [ps], [a_sg])
            for bi, blk in enumerate(LATB):
                lc = slice(blk[0] - 256, blk[1] - 256)
                qv = a_qr[:].rearrange("p (a g q) -> p a g q", a=4, g=4)
                for g in range(4):
                    ps = next_pp()
                    proj_feat(wC, g * 64, 64, blk, ps)
                    P.op("dve", C.scalar_tensor_tensor(out=tmpf[0:64, 0:512], in0=ps[0:64, 0:512], scalar=0.125, in1=a_cos[:, lc], op0=ALU.mult, op1=ALU.mult),
                         [ps, a_cos], [tmpf])
                    ps2 = next_pp()
                    proj_feat(wD, g * 64, 64, blk, ps2)
                    P.op("dve", C.scalar_tensor_tensor(out=tmpf2[0:64, 0:512], in0=ps2[0:64, 0:512], scalar=0.125, in1=a_sin[:, lc], op0=ALU.mult, op1=ALU.mult),
                         [ps2, a_sin], [tmpf2])
                    P.op("pool", C.tensor_tensor(out=qv[:, :, g, :], in0=tmpf[0:64, 0:512].rearrange("p (a q) -> p a q", a=4),
                                                 in1=tmpf2[0:64, 0:512].rearrange("p (a q) -> p a q", a=4), op=ALU.add), [tmpf, tmpf2], [a_qr])
                for qbl in range(4):
                    qb = 4 * bi + qbl
                    tiles = [(0, None), (1, None)]
                    if qb > 0:
                        tiles.append((2 + qb - 1, maskP))
                    tiles.append((2 + qb, None))
                    if qb < 15:
                        tiles.append((2 + qb + 1, maskN))
                    pts = []
                    for (ti, msk) in tiles:
                        pss = banks[sbank[0] % 4]; sbank[0] += 1
                        pt = PT[pti[0] % 8]; pti[0] += 1
                        P.op("pe", C.matmul(pss[:, 0:512], lhsT=a_kT[:, ti * 128:(ti + 1) * 128], rhs=a_qr[:, qbl * 512:(qbl + 1) * 512], start=True, stop=True),
                             [a_kT, a_qr], [pss])
                        P.op("act", C.activation(out=pt[:], in_=pss[:, 0:512], func=AF.Exp), [pss], [pt])
                        if msk is not None:
                            P.op("dve", C.tensor_tensor(out=pt[:].rearrange("p (g q) -> p g q", g=4), in0=pt[:].rearrange("p (g q) -> p g q", g=4),
                                                        in1=msk[:].unsqueeze(1).to_broadcast([128, 4, 128]), op=ALU.mult), [pt, msk], [pt])
                        pts.append((ti, pt))
                    psA, psB = banks[4], banks[5]
                    for (pso, isv) in ((psA, True), (psB, False)):
                        for g in range(4):
                            for n_, (ti, pt) in enumerate(pts):
                                lh = a_vt[:, ti, :] if isv else onesK[:]
                                P.op("pe", C.matmul(pso[(g % 2) * 64:(g % 2) * 64 + 64, (g // 2) * 128:(g // 2) * 128 + 128], lhsT=lh,
                                                    rhs=pt[:, g * 128:(g + 1) * 128], start=(n_ == 0), stop=(n_ == len(pts) - 1)),
                                     [a_vt if isv else onesK, pt], [pso])
                    for gh in range(2):
                        P.op("dve", C.tensor_scalar(out=tmpf[:, gh * 128:(gh + 1) * 128], in0=psB[:, gh * 128:(gh + 1) * 128],
                                                    scalar1=esk[:, kvh * 2 + gh:kvh * 2 + gh + 1], scalar2=None, op0=ALU.add), [psB, esk], [tmpf])
                    P.op("dve", C.reciprocal(out=tmpf[:, 0:256], in_=tmpf[:, 0:256]), [tmpf], [tmpf])
                    P.op("dve", C.tensor_tensor(out=tmpf2[:, 0:256], in0=psA[:, 0:256], in1=tmpf[:, 0:256], op=ALU.mult), [psA, tmpf], [tmpf2])
                    P.op("pool", C.tensor_tensor(out=astage[:, 0:256], in0=tmpf2[:, 0:256], in1=a_sg[:, qb, :], op=ALU.mult), [tmpf2, a_sg], [astage])
                    for gh in range(2):
                        P.dma("sp", aT_d, aT_d[kvh * 2 + gh, :, 256 + qb * 128:256 + (qb + 1) * 128], astage, astage[:, gh * 128:(gh + 1) * 128])

    zres = [carve("zres%d" % i, ap_) for i, ap_ in enumerate([
        qT[:, 0:2048], kT[0][:, 0:2048], kT[1][:, 0:2048],
        ktok[0][:].rearrange("p a b -> p (a b)")[:, 0:2048], ktok[1][:].rearrange("p a b -> p (a b)")[:, 0:2048],
        vtok[:].rearrange("p a b -> p (a b)")[:, 0:2048],
        wC[:].rearrange("p a b -> p (a b)"), wD[:].rearrange("p a b -> p (a b)")])]

    def glu_layer(b):
        for kt in range(8):
            P.dma("sp", zres[kt], zres[kt][:], zT_d, zT_d[kt, :, 256:2304])

        def zproj(wt, blk, out_ps):
            c0, c1 = blk[0] - 256, blk[1] - 256
            for kt in range(8):
                P.op("pe", C.matmul(out_ps[:, 0:512], lhsT=wt[:, kt, 0:128], rhs=zres[kt][:, c0:c1], start=(kt == 0), stop=(kt == 7)),
                     [wt, zres[kt]], [out_ps])
        for jt in range(8):
            load_w(wA, glu_w, jt * 128, 128)
            load_w(wB, glu_w, 1024 + jt * 128, 128)
            load_w(wE, od_w_in, 3584 + jt * 128, 128)
            for blk in LATB:
                psa = next_pp()
                zproj(wA, blk, psa)
                psb = next_pp()
                zproj(wB, blk, psb)
                psg = next_pp()
                proj_feat(wE, 0, 128, blk, psg)
                P.op("act", C.activation(out=tmpf[:], in_=psb[:, 0:512], func=AF.Sigmoid), [psb], [tmpf])
                P.op("act", C.activation(out=sgate[:], in_=psg[:, 0:512], func=AF.Sigmoid), [psg], [sgate])
                P.op("dve", C.tensor_tensor(out=tmpf2[:], in0=psa[:, 0:512], in1=tmpf[:], op=ALU.mult), [psa, tmpf], [tmpf2])
                P.op("dve", C.tensor_tensor(out=sgate[:], in0=psg[:, 0:512], in1=sgate[:], op=ALU.mult), [psg, sgate], [sgate])
                P.op("dve", C.tensor_tensor(out=astage[:], in0=tmpf2[:], in1=sgate[:], op=ALU.mult), [tmpf2, sgate], [astage])
                P.dma("sp", aT_d, aT_d[8 + jt, :, blk[0]:blk[1]], astage, astage[:])

    fin_t = []

    def final_layer(b):
        if not fin_t:
            fin_t.append(T(P, "f_x2", ws[:, 0:4096].rearrange("p (k n) -> p k n", k=8), "sbuf"))
            fin_t.append(T(P, "f_sq0", gt0f[:, 0:2048].rearrange("p (k n) -> p k n", k=4), "sbuf"))
            fin_t.append(T(P, "f_sq1", gt1f[:, 0:2048].rearrange("p (k n) -> p k n", k=4), "sbuf"))
        x2, sq0, sq1 = fin_t
        load_w(wout, od_w_out, 0, 1024, 0, 16)
        for (c0, c1) in LATB:
            N = 512
            for kt in range(16):
                P.dma("sp", akt[kt], akt[kt][:, 0:N], aT_d, aT_d[kt, :, c0:c1])
            for jt in range(8):
                ps = next_pp()
                for kt in range(16):
                    P.op("pe", C.matmul(ps[:, 0:N], lhsT=wout[:, kt, jt * 128:(jt + 1) * 128], rhs=akt[kt][:, 0:N],
                                        start=(kt == 0), stop=(kt == 15)), [wout, akt[kt]], [ps])
                P.dma("sp", xres, xres[:, 0:N], x1T, x1T[b, jt, :, c0:c1])
                P.op("dve", C.scalar_tensor_tensor(out=x2[:, jt, :], in0=ps[:, 0:N], scalar=mod[:, 16 + jt, b:b + 1],
                                                   in1=xres[:, 0:N], op0=ALU.mult, op1=ALU.add), [ps, mod, xres], [x2])
                sq = sq0 if jt < 4 else sq1
                P.op("act", C.activation(out=sq[:, jt % 4, :], in_=x2[:, jt, :], func=AF.Square), [x2], [sq])
            ps = next_pp()
            for kt in range(8):
                sq = sq0 if kt < 4 else sq1
                P.op("pe", C.matmul(ps[:, 0:N], lhsT=ones[:], rhs=sq[:, kt % 4, :], start=(kt == 0), stop=(kt == 7)), [ones, sq], [ps])
            rstd_from_ps(ps, N, 1024.0)
            for kt in range(8):
                P.op("dve", C.scalar_tensor_tensor(out=xo[:, 0:N], in0=x2[:, kt, :], scalar=fng[:, kt:kt + 1], in1=rstd[:, 0:N],
                                                   op0=ALU.mult, op1=ALU.mult), [x2, fng, rstd], [xo])
                P.dma("sp", outT, outT[b, kt, :, c0 - 256:c1 - 256], xo, xo[:, 0:N])

    def layer1(b):
        norm_mod(x1T, b)
        P.barrier()
        s5_layer(b)
        P.barrier()
        attn_layer(b)
        P.barrier()
        glu_layer(b)
        P.barrier()
        final_layer(b)
        P.barrier()

    for d in range(2):
        P.dma("pool", gkw_s[d], gkw_s[d][:], gkw, gkw[d])
        P.dma("sp", gkb_s[d], gkb_s[d][:], gkb, gkb[d])
    P.dma("sp", glang_s, glang_s[:], glang, glang[:])
    P.dma("sp", hgng_s, hgng_s[:], hgng, hgng[:])
    P.dma("sp", lbtmp, lbtmp[:, 0, :], lbraw, lbraw[0, 0:1, :].partition_broadcast(128))
    P.dma("sp", lbtmp, lbtmp[:, 1, :], lbraw, lbraw[1, 0:1, :].partition_broadcast(128))
    P.dma("sp", oml_bc, oml_bc[:, 0, :], lbraw, lbraw[0, 1:2, :].partition_broadcast(128))
    P.dma("sp", oml_bc, oml_bc[:, 1, :], lbraw, lbraw[1, 1:2, :].partition_broadcast(128))
    P.op("act", C.activation(out=lbtmp[:], in_=lbtmp[:], func=AF.Exp), [lbtmp], [lbtmp])
    P.op("act", C.activation(out=oml_bc[:], in_=oml_bc[:], func=AF.Exp), [oml_bc], [oml_bc])
    P.op("dve", C.tensor_tensor(out=lb_bc[:], in0=lbtmp[:], in1=oml_bc[:], op=ALU.add), [lbtmp, oml_bc], [lb_bc])
    P.op("dve", C.reciprocal(out=lb_bc[:], in_=lb_bc[:]), [lb_bc], [lb_bc])
    P.op("dve", C.tensor_tensor(out=oml_bc[:], in0=oml_bc[:], in1=lb_bc[:], op=ALU.mult), [oml_bc, lb_bc], [oml_bc])
    P.op("dve", C.tensor_tensor(out=lb_bc[:], in0=lbtmp[:], in1=lb_bc[:], op=ALU.mult), [lbtmp, lb_bc], [lb_bc])
    P.dma("sp", lbT, lbT[:], lbrawT, lbrawT[:])
    P.op("act", C.activation(out=lbT[:], in_=lbT[:], func=AF.Exp), [lbT], [lbT])
    P.op("dve", C.tensor_tensor(out=lbTt[:], in0=lbT[:, :, 0, :], in1=lbT[:, :, 1, :], op=ALU.add), [lbT], [lbTt])
    P.op("dve", C.reciprocal(out=lbTt[:], in_=lbTt[:]), [lbTt], [lbTt])
    P.op("dve", C.tensor_tensor(out=omlT[:], in0=lbT[:, :, 1, :], in1=lbTt[:], op=ALU.mult), [lbT, lbTt], [omlT])

    P.dma("pool", maskP, maskP[:], maskP_d, maskP_d[:]); P.dma("pool", maskN, maskN[:], maskN_d, maskN_d[:])
    P.dma("sp", esk, esk[:], sinkT, sinkT[:]); P.dma("sp", fng, fng[:], fngT, fngT[:])
    P.op("act", C.activation(out=esk[:], in_=esk[:], func=AF.Exp), [esk], [esk])
    P.op("pool", C.memset(onesK[:], 1.0), [], [onesK])
    P.barrier()
    s5_prep()
    P.barrier()
    for L in range(2):
        mod, msc = modL[L], mscL[L]
        adaln(L)
        P.barrier()
    outs = [outT]
    if stop_after == 0:
        outs = [x1T, outT]
    for b in range(2):
        mod, msc = modL[0], mscL[0]
        layer0(b)
        if stop_after == 0:
            continue
        mod, msc = modL[1], mscL[1]
        layer1(b)
    P.wait_all("sp", outs)
    P.barrier()
    P.emit()
    st.close()
    return nc, P


def host_inputs(inp, core):
    f = np.float32
    b0 = 2 * core
    out = {}
    xcat = np.concatenate([inp["ctx"][b0:b0 + 2], inp["x"][b0:b0 + 2]], axis=1)
    out["xT"] = np.ascontiguousarray(xcat.transpose(0, 2, 1).reshape(2, 8, 128, NT)).astype(f)
    cv = np.stack([inp["c"][b0], inp["c"][b0 + 1], inp["c_ctx"]], axis=1)
    out["cT"] = np.ascontiguousarray(cv.reshape(8, 128, 3).transpose(1, 0, 2)).astype(f)
    out["ada_w"] = np.ascontiguousarray(inp["ada_w"]).astype(f)
    out["ada_bT"] = np.ascontiguousarray(inp["ada_b"].reshape(2, 24, 128).transpose(0, 2, 1)).astype(f)
    out["normgT"] = np.ascontiguousarray(inp["norm_g"].reshape(2, 8, 128).transpose(0, 2, 1)).astype(f)
    out["fngT"] = np.ascontiguousarray(inp["final_norm_g"].reshape(8, 128).T).astype(f)
    out["ev_w_in"] = np.ascontiguousarray(inp["ev_w_in"][0]).astype(f)
    out["ev_w_out"] = np.ascontiguousarray(inp["ev_w_out"][0]).astype(f)
    out["gkw"] = np.ascontiguousarray(inp["gla_gk_w"][0]).astype(f)
    out["gkb"] = np.ascontiguousarray(inp["gla_gk_b"][0].reshape(2, 1, 512)).astype(f)
    out["glang"] = np.ascontiguousarray(inp["gla_norm_g"][0].reshape(2, 128).T).astype(f)
    out["hgng"] = np.ascontiguousarray(inp["hgrn_norm_g"][0].reshape(128, 1)).astype(f)
    out["lbraw"] = np.ascontiguousarray(inp["hgrn_lb_raw"]).astype(f)
    out["lbrawT"] = np.ascontiguousarray(inp["hgrn_lb_raw"].reshape(2, 2, 8, 128).transpose(3, 0, 1, 2)).astype(f)
    for k, v in host_consts().items():
        out["c_" + k] = v
    w1 = np.asarray(inp["od_w_in"][0], f)
    out["od_w_in"] = np.ascontiguousarray(w1)
    perm = np.arange(1280)
    dd = perm % 32
    perm = np.where(dd < 16, perm + 16, perm - 16)
    out["od_w_sw"] = np.ascontiguousarray(w1[:, perm])
    out["od_w_out"] = np.ascontiguousarray(inp["od_w_out"][0]).astype(f)
    out["glu_w"] = np.ascontiguousarray(inp["s5_glu_w"][0]).astype(f)
    t = np.arange(2048)
    freqs = 10000.0 ** (-np.arange(16, dtype=np.float64) / 16.0)
    d = np.arange(64)
    pos = np.where((d // 32)[:, None] == 0, (t // 64)[None, :], (t % 64)[None, :]).astype(np.float64)
    ang = pos * freqs[d % 16][:, None]
    sgn = np.where((d % 32) < 16, -1.0, 1.0)[:, None]
    out["ropeC"] = np.cos(ang).astype(f)
    out["ropeS"] = (np.sin(ang) * sgn).astype(f)
    sk = np.asarray(inp["attn_sink"][0], f)
    p = np.arange(128)
    st = np.zeros((128, 8), f)
    for kvh in range(4):
        for gh in range(2):
            st[:, kvh * 2 + gh] = sk[kvh * 4 + 2 * gh + p // 64]
    out["sinkT"] = st
    kk = np.arange(128)[:, None]; qq = np.arange(128)[None, :]
    out["maskP"] = (kk >= qq).astype(f)
    out["maskN"] = (kk <= qq).astype(f)
    def lamT(a):
        return np.ascontiguousarray(np.asarray(a, f).reshape(2, 32, 2, 64).transpose(2, 3, 0, 1).reshape(128, 64))
    out["lamTre"] = lamT(inp["s5_lambda_re"][0])
    out["lamTim"] = lamT(inp["s5_lambda_im"][0])
    ld = np.asarray(inp["s5_log_dt"][0], f).reshape(2, 32, 2)
    out["ldtT"] = np.ascontiguousarray(np.repeat(ld.transpose(2, 0, 1).reshape(2, 1, 64), 64, axis=1).reshape(128, 64))
    def bT(a):
        return np.ascontiguousarray(np.asarray(a, f).reshape(32, 2, 64, 16).transpose(1, 2, 0, 3).reshape(128, 32, 16))
    def cTt(a):
        return np.ascontiguousarray(np.asarray(a, f).reshape(32, 2, 16, 64).transpose(1, 3, 0, 2).reshape(128, 32, 16))
    out["BTre"] = bT(inp["s5_b_re"][0]); out["BTim"] = bT(inp["s5_b_im"][0])
    out["CTre"] = cTt(inp["s5_c_re"][0]); out["CTim"] = cTt(inp["s5_c_im"][0])
    out["dskT"] = np.ascontiguousarray(np.asarray(inp["s5_d"][0], f).reshape(8, 128).T)
    out["iota"] = np.ascontiguousarray(np.broadcast_to(np.arange(1152, dtype=f)[None, :], (128, 1152)))
    return out


def kernel(**inp):
    inp = {k: np.asarray(v) for k, v in inp.items()}
    nc, P = build_program()
    in_maps = [host_inputs(inp, c) for c in range(8)]
    res = run_bass_kernel_spmd(nc, in_maps, core_ids=list(range(8)))
    outs = []
    for c in range(8):
        o = res.results[c]["outT"]
        outs.append(o.reshape(2, 1024, 2048).transpose(0, 2, 1))
    return np.ascontiguousarray(np.concatenate(outs, axis=0)).astype(np.float32)
```
